# Optimizing a Trainium2 kernel written in Bass

```python
import math
import jax, jax.numpy as jnp
from jax import lax
import numpy as np

D_MODEL = 1024
BATCH = 32
SEQ = 2048
DEPTH = 2

MEM_LEN = 256
CHUNK = 64
Q_BLOCK = 128
ROPE_THETA = 10000.0
EPS = 1e-6
NEG_BIG = -1e30

DIFF_HEADS = 8
DIFF_HD = 64
DIFF_VD = 2 * DIFF_HD
DIFF_W = DIFF_HEADS * DIFF_VD

LRU_W = D_MODEL
LRU_BLOCKS = 8
LRU_BW = LRU_W // LRU_BLOCKS
CONV_W = 4
RG_C = 8.0

XATTN_HEADS = 4
XATTN_HD = D_MODEL // XATTN_HEADS
XATTN_W = XATTN_HEADS * XATTN_HD

N_BRANCH = 3
D_FF = 4 * D_MODEL

IN_SPLITS = (
    2 * DIFF_HEADS * DIFF_HD,
    2 * DIFF_HEADS * DIFF_HD,
    DIFF_W,
    LRU_W,
    LRU_W,
    XATTN_W,
    N_BRANCH * D_MODEL,
)
N_IN = sum(IN_SPLITS)

kernel_name = 'hybrid_diffattn_rglru_memxattn_gated'


def rms_norm(x, g):
    xf = x.astype(jnp.float32)
    y = xf * lax.rsqrt(jnp.mean(xf * xf, axis=-1, keepdims=True) + EPS)
    return (y * g.astype(jnp.float32)).astype(x.dtype)


def in_projection(h, w):
    outs = []
    start = 0
    for n in IN_SPLITS:
        outs.append(h @ w[:, start:start + n])
        start += n
    return outs


def rope_tables(seq_len):
    half = DIFF_HD // 2
    pos = jnp.arange(seq_len, dtype=jnp.float32)
    inv = 1.0 / (ROPE_THETA ** (jnp.arange(half, dtype=jnp.float32) / half))
    ang = pos[:, None] * inv[None, :]
    return jnp.cos(ang), jnp.sin(ang)


def apply_rope(t, cos, sin):
    half = DIFF_HD // 2
    c = cos[:, None, None, :]
    s = sin[:, None, None, :]
    tf = t.astype(jnp.float32)
    t1, t2 = tf[..., :half], tf[..., half:]
    return jnp.concatenate([t1 * c - t2 * s, t2 * c + t1 * s], axis=-1).astype(t.dtype)


def diff_attention(q, k, v, lam, subln_g, lam_init):
    B, S = q.shape[0], q.shape[1]
    nq = S // Q_BLOCK
    scale = DIFF_HD ** -0.5
    qt = q.transpose(3, 0, 2, 1, 4)
    kt = k.transpose(3, 0, 2, 1, 4)
    k1, k2 = kt[0], kt[1]
    vt = v.transpose(0, 2, 1, 3)
    qblocks = qt.reshape(2, B, DIFF_HEADS, nq, Q_BLOCK, DIFF_HD).transpose(3, 0, 1, 2, 4, 5)
    key_chunk = jnp.arange(S) // CHUNK

    def block(args):
        qb, idx = args
        q_chunk = (idx * Q_BLOCK + jnp.arange(Q_BLOCK)) // CHUNK
        mask = key_chunk[None, :] <= q_chunk[:, None]
        s1 = jnp.einsum('bhqd,bhkd->bhqk', qb[0], k1).astype(jnp.float32) * scale
        s2 = jnp.einsum('bhqd,bhkd->bhqk', qb[1], k2).astype(jnp.float32) * scale
        p1 = jax.nn.softmax(jnp.where(mask, s1, NEG_BIG), axis=-1)
        p2 = jax.nn.softmax(jnp.where(mask, s2, NEG_BIG), axis=-1)
        p = p1 - lam * p2
        return jnp.einsum('bhqk,bhkv->bhqv', p.astype(vt.dtype), vt)

    out = lax.map(block, (qblocks, jnp.arange(nq)))
    out = out.transpose(1, 0, 3, 2, 4).reshape(B, S, DIFF_HEADS, DIFF_VD)
    out = rms_norm(out, subln_g) * (1.0 - lam_init)
    return out.reshape(B, S, DIFF_W)


def causal_conv(x, w, b):
    S = x.shape[1]
    xp = jnp.pad(x, ((0, 0), (CONV_W - 1, 0), (0, 0)))
    y = xp[:, 0:S] * w[0]
    for t in range(1, CONV_W):
        y = y + xp[:, t:t + S] * w[t]
    return y + b


def rg_lru(x, wa, ba, wi, bi, lam):
    B, S, W = x.shape
    xf = x.astype(jnp.float32)
    xb = xf.reshape(B, S, LRU_BLOCKS, LRU_BW)
    r = jax.nn.sigmoid(jnp.einsum('bsnc,ncd->bsnd', xb, wa.astype(jnp.float32)).reshape(B, S, W)
                       + ba.astype(jnp.float32))
    i = jax.nn.sigmoid(jnp.einsum('bsnc,ncd->bsnd', xb, wi.astype(jnp.float32)).reshape(B, S, W)
                       + bi.astype(jnp.float32))
    log_a = -RG_C * r * jax.nn.softplus(-lam.astype(jnp.float32))
    a = jnp.exp(log_a)
    u = jnp.sqrt(-jnp.expm1(2.0 * log_a)) * (i * xf)

    def combine(left, right):
        a_l, u_l = left
        a_r, u_r = right
        return a_l * a_r, a_r * u_l + u_r

    _, hs = lax.associative_scan(combine, (a, u), axis=1)
    return hs.astype(x.dtype)


def memory_attention(zq, mem, g_mem, w_kv):
    B, S = zq.shape[0], zq.shape[1]
    M = mem.shape[1]
    q = zq.reshape(B, S, XATTN_HEADS, XATTN_HD)
    kv = (rms_norm(mem, g_mem) @ w_kv).reshape(B, M, 2, XATTN_HEADS, XATTN_HD)
    s = jnp.einsum('bshd,bmhd->bhsm', q, kv[:, :, 0]).astype(jnp.float32) * (XATTN_HD ** -0.5)
    p = jax.nn.softmax(s, axis=-1)
    o = jnp.einsum('bhsm,bmhd->bshd', p.astype(q.dtype), kv[:, :, 1])
    return o.reshape(B, S, XATTN_W)


def setup_inputs(seed: int = 0) -> dict:
    key = jax.random.key(seed)
    ks = iter(jax.random.split(key, 40))
    f32 = jnp.float32

    def nrm(shape, scale):
        return jax.random.normal(next(ks), shape, f32) * scale

    def gain(shape):
        return 1.0 + 0.05 * jax.random.normal(next(ks), shape, f32)

    x = jax.random.normal(next(ks), (BATCH, SEQ, D_MODEL), f32)
    mem = jax.random.normal(next(ks), (BATCH, MEM_LEN, D_MODEL), f32)
    u = jax.random.uniform(next(ks), (DEPTH, LRU_W), f32, minval=0.9, maxval=0.999)
    a0 = u ** (1.0 / RG_C)
    rg_lambda = jnp.log(a0) - jnp.log1p(-a0)
    return {
        'x': x,
        'mem': mem,
        'norm_mix_pre': gain((DEPTH, D_MODEL)),
        'norm_mix_post': gain((DEPTH, D_MODEL)),
        'norm_mlp_pre': gain((DEPTH, D_MODEL)),
        'norm_mlp_post': gain((DEPTH, D_MODEL)),
        'norm_mem': gain((DEPTH, D_MODEL)),
        'w_in': nrm((DEPTH, D_MODEL, N_IN), D_MODEL ** -0.5),
        'b_gate': nrm((DEPTH, N_BRANCH, D_MODEL), 0.1),
        'lambda_q1': nrm((DEPTH, DIFF_HD), 0.1),
        'lambda_k1': nrm((DEPTH, DIFF_HD), 0.1),
        'lambda_q2': nrm((DEPTH, DIFF_HD), 0.1),
        'lambda_k2': nrm((DEPTH, DIFF_HD), 0.1),
        'diff_subln': gain((DEPTH, DIFF_VD)),
        'conv_w': nrm((DEPTH, CONV_W, LRU_W), CONV_W ** -0.5),
        'conv_b': nrm((DEPTH, LRU_W), 0.02),
        'rg_wa': nrm((DEPTH, LRU_BLOCKS, LRU_BW, LRU_BW), LRU_BW ** -0.5),
        'rg_ba': nrm((DEPTH, LRU_W), 0.1),
        'rg_wi': nrm((DEPTH, LRU_BLOCKS, LRU_BW, LRU_BW), LRU_BW ** -0.5),
        'rg_bi': nrm((DEPTH, LRU_W), 0.1),
        'rg_lambda': rg_lambda,
        'w_kv_mem': nrm((DEPTH, D_MODEL, 2 * XATTN_W), D_MODEL ** -0.5),
        'w_proj_attn': nrm((DEPTH, DIFF_W, D_MODEL), DIFF_W ** -0.5),
        'w_proj_lru': nrm((DEPTH, LRU_W, D_MODEL), LRU_W ** -0.5),
        'w_proj_mem': nrm((DEPTH, XATTN_W, D_MODEL), XATTN_W ** -0.5),
        'w_out': nrm((DEPTH, D_MODEL, D_MODEL), D_MODEL ** -0.5),
        'w_mlp_up': nrm((DEPTH, D_MODEL, D_FF), D_MODEL ** -0.5),
        'w_mlp_down': nrm((DEPTH, D_FF, D_MODEL), D_FF ** -0.5),
    }


def reference(x, mem, norm_mix_pre, norm_mix_post, norm_mlp_pre, norm_mlp_post, norm_mem,
              w_in, b_gate, lambda_q1, lambda_k1, lambda_q2, lambda_k2, diff_subln,
              conv_w, conv_b, rg_wa, rg_ba, rg_wi, rg_bi, rg_lambda, w_kv_mem,
              w_proj_attn, w_proj_lru, w_proj_mem, w_out, w_mlp_up, w_mlp_down):
    B, S, _ = x.shape
    cos, sin = rope_tables(S)
    for l in range(DEPTH):
        h = rms_norm(x, norm_mix_pre[l])
        zq, zk, zv, zx, zy, zxq, zg = in_projection(h, w_in[l])

        q = apply_rope(zq.reshape(B, S, DIFF_HEADS, 2, DIFF_HD), cos, sin)
        k = apply_rope(zk.reshape(B, S, DIFF_HEADS, 2, DIFF_HD), cos, sin)
        v = zv.reshape(B, S, DIFF_HEADS, DIFF_VD)
        lam_init = 0.8 - 0.6 * math.exp(-0.3 * l)
        lam = (jnp.exp(jnp.sum(lambda_q1[l].astype(jnp.float32) * lambda_k1[l].astype(jnp.float32)))
               - jnp.exp(jnp.sum(lambda_q2[l].astype(jnp.float32) * lambda_k2[l].astype(jnp.float32)))
               + lam_init)
        y_attn = diff_attention(q, k, v, lam, diff_subln[l], lam_init)

        xc = causal_conv(zx, conv_w[l], conv_b[l])
        y_lru = rg_lru(xc, rg_wa[l], rg_ba[l], rg_wi[l], rg_bi[l], rg_lambda[l]) * jax.nn.gelu(zy)

        y_mem = memory_attention(zxq, mem, norm_mem[l], w_kv_mem[l])

        gates = jax.nn.sigmoid(zg.reshape(B, S, N_BRANCH, D_MODEL) + b_gate[l])
        merged = (gates[:, :, 0] * (y_attn @ w_proj_attn[l])
                  + gates[:, :, 1] * (y_lru @ w_proj_lru[l])
                  + gates[:, :, 2] * (y_mem @ w_proj_mem[l]))
        x = x + rms_norm(merged @ w_out[l], norm_mix_post[l])

        h = rms_norm(x, norm_mlp_pre[l])
        a = jax.nn.relu(h @ w_mlp_up[l])
        x = x + rms_norm((a * a) @ w_mlp_down[l], norm_mlp_post[l])
    return x
```

```python
import math
import numpy as np
import concourse.bass as bass
import concourse.mybir as mybir
from concourse.bass_utils import run_bass_kernel_spmd

F32 = mybir.dt.float32
BF16 = mybir.dt.bfloat16
AF = mybir.ActivationFunctionType
ALU = mybir.AluOpType
AX = mybir.AxisListType

P = 128
D = 1024
TT = 512
NH = 8
MEM = 256
NIN = 9216
DFF = 4096
EPS = 1e-6
NSLOT = 3
NV = 116
NTILES = 50
SAME_ENGINE_SYNC = True

T_IN, T_KV, T_PA, T_PL, T_PM, T_OUT, T_UP, T_DN = 0, 18, 22, 24, 26, 28, 30, 38


class Op:
    __slots__ = ("eng", "fn", "deps", "dma_sem", "signal", "val", "idx", "waits", "extra")

    def __init__(self, eng, fn, deps, dma_sem=None):
        self.eng = eng
        self.fn = fn
        self.deps = deps
        self.dma_sem = dma_sem
        self.signal = False
        self.val = 0
        self.idx = 0
        self.waits = []
        self.extra = []


class Prog:
    ENGS = ("pe", "act", "dve", "pool", "sp")

    def __init__(self):
        self.ops = []
        self.last_write = {}
        self.readers = {}

    def emit(self, eng, fn, reads=(), writes=(), dma_sem=None, extra=()):
        deps = []
        for c in reads:
            w = self.last_write.get(c)
            if w is not None:
                deps.append(w)
        for c in writes:
            w = self.last_write.get(c)
            if w is not None:
                deps.append(w)
            deps.extend(self.readers.get(c, ()))
        deps.extend(extra)
        op = Op(eng, fn, deps, dma_sem)
        self.ops.append(op)
        for c in writes:
            self.last_write[c] = op
            self.readers[c] = []
        for c in reads:
            self.readers.setdefault(c, []).append(op)
        return op

    def finalize(self):
        cnt = {e: 0 for e in self.ENGS}
        for op in self.ops:
            cnt[op.eng] += 1
            op.idx = cnt[op.eng]
        waited = {e: {} for e in self.ENGS}
        dma_cnt = {}
        for op in self.ops:
            if op.dma_sem is not None:
                dma_cnt[op.dma_sem] = dma_cnt.get(op.dma_sem, 0) + 1
                op.val = 16 * dma_cnt[op.dma_sem]
        for op in self.ops:
            need = {}
            for d in op.deps:
                if d.dma_sem is not None:
                    key = ("dma", d.dma_sem)
                    v = d.val
                else:
                    if d.eng == op.eng and (d.eng == "pe" or d.eng == "sp" or not SAME_ENGINE_SYNC):
                        continue
                    key = ("eng", d.eng)
                    v = d.idx
                if need.get(key, (0, None))[0] < v:
                    need[key] = (v, d)
            for key, (v, d) in need.items():
                if waited[op.eng].get(key, 0) >= v:
                    continue
                waited[op.eng][key] = v
                if key[0] == "eng":
                    d.signal = True
                op.waits.append(d)
        sig = {e: 0 for e in self.ENGS}
        for op in self.ops:
            if op.dma_sem is None and op.signal:
                sig[op.eng] += 1
                op.val = sig[op.eng]
        return dma_cnt


def build(nseq, S, layers, nlayers_total):
    NT = S // TT
    NB = S // P
    nc = bass.Bass("TRN2", target_bir_lowering=False)
    L = nlayers_total

    def din(name, shape, dt=F32):
        return nc.dram_tensor(name, list(shape), dt, kind="ExternalInput").ap()

    x_d = din("x", [nseq * S, D])
    mem_d = din("mem", [nseq * MEM, D])
    w_in_d = din("w_in", [L, D, NIN])
    w_kv_d = din("w_kv", [L, D, 2 * D])
    w_sw_d = din("w_sw", [L, D, 2 * D])
    w_pa_d = din("w_pa", [L, D, D])
    w_pl_d = din("w_pl", [L, D, D])
    w_pm_d = din("w_pm", [L, D, D])
    w_out_d = din("w_out", [L, D, D])
    w_up_d = din("w_up", [L, D, DFF])
    w_dn_d = din("w_dn", [L, DFF, D])
    rg_d = din("rg_w", [L, 2, 8, P, P])
    pvec_d = din("pvec", [L, P, NV])
    gpost_d = din("gpost", [L, 2, P, D])
    lamv_d = din("lamv", [L, P, 4 * 64])
    rope_d = din("rope", [2, P, S])
    cmat_d = din("cmat", [2, P, P])
    out_d = nc.dram_tensor("out", [nseq * S, D], F32, kind="ExternalOutput").ap()
    xmid_d = nc.dram_tensor("xmid", [nseq * S, D], F32, kind="Internal").ap()
    wsc_d = nc.dram_tensor("wsc", [L, NTILES, P, 8 * 512], BF16, kind="Internal").ap()
    rgsc_d = nc.dram_tensor("rgsc", [L, P, 2 * 8 * P], BF16, kind="Internal").ap()

    pg = Prog()
    E = pg.emit

    from contextlib import ExitStack
    es = ExitStack()

    def sb(name, shape, dt):
        return es.enter_context(nc.sbuf_tensor("sb_" + name, list(shape), dt))

    with es:
        KT = sb("KT", [P, NH, S], BF16)
        VC = sb("VC", [P, NB, NH, 130], BF16)
        xt = sb("xt", [P, 4, D], F32)
        HY = sb("HY", [P, 16 * 512], BF16)
        BIG = sb("BIG", [P, 32 * 512], BF16)
        WS = [sb(f"ws{i}", [P, 8, 512], BF16) for i in range(NSLOT)]
        KmT = sb("KmT", [P, 8, MEM], BF16)
        Vm = sb("Vm", [P, 2, D], BF16)
        ropet = sb("ropet", [P, 2, TT], F32)
        rgw = sb("rgw", [P, 2, 8, P], BF16)
        pv = sb("pv", [P, L, NV], F32)
        cmat_f = sb("cmat_f", [P, 2, P], F32)
        cmat = sb("cmat", [P, 2, P], BF16)
        ones_bf = sb("ones_bf", [P, P], BF16)
        der = sb("der", [P, L, 64], F32)
        small = sb("small", [P, 64], F32)
        junk = sb("junk", [P, D], BF16)
        qb_t = [sb(f"qb{i}", [P, TT], BF16) for i in range(2)]
        t1_t = [sb(f"t1_{i}", [P, TT], F32) for i in range(2)]
        t2_t = [sb(f"t2_{i}", [P, TT], F32) for i in range(2)]
        PT = [[sb(f"pt{i}_{m}", [P, TT], BF16) for m in range(2)] for i in range(2)]
        o_t = [sb(f"o_{i}", [P, P], F32) for i in range(2)]
        tt_t = [sb(f"tt_{i}", [P, P], F32) for i in range(2)]
        ybf = sb("ybf", [P, 4, P], BF16)
        zxe = sb("zxe", [P, TT + 3], F32)
        xc = sb("xc", [P, TT], F32)
        xcb = sb("xcb", [P, TT], BF16)
        thr, thi, a_t, a2_t = t1_t[0], t1_t[1], t2_t[0], t2_t[1]
        HG = sb("HG", [P, 2 * TT], F32)
        hs_t = HG[:, 0:TT]
        gz_t = HG[:, TT:2 * TT]
        gp = HG
        cz = sb("cz", [P, 8, 3], F32)
        hcar = sb("hcar", [P, 8], F32)
        assert L == 2
        lamv = xc[:, :].rearrange("p (l v) -> p l v", l=L)

        ps = [es.enter_context(nc.psum_tensor(f"psum{i}", [P, 512], F32)) for i in range(7)]
        pT = es.enter_context(nc.psum_tensor("psum7", [P, 1024], BF16))

        hT = HY[:, 0:8 * 512].rearrange("p (c n) -> p c n", c=8)
        yT = HY[:, 8 * 512:16 * 512].rearrange("p (c n) -> p c n", c=8)
        o_raw = HY[:, :].bitcast(F32).rearrange("p (b d) -> p b d", b=4)
        aT = BIG[:, :].rearrange("p (f n) -> p f n", f=32)
        m_f = BIG[:, 0:16 * 512].bitcast(F32).rearrange("p (c n) -> p c n", c=8)
        m_bf = BIG[:, 16 * 512:24 * 512].rearrange("p (c n) -> p c n", c=8)
        xn = BIG[:, 16 * 512:24 * 512].rearrange("p (b d) -> p b d", b=4)
        mt_f = BIG[:, 16 * 512:24 * 512].bitcast(F32).rearrange("p (b d) -> p b d", b=2)
        QT = BIG[:, 24 * 512:32 * 512].rearrange("p (c n) -> p c n", c=8)
        memn = BIG[:, 24 * 512:28 * 512].rearrange("p (b d) -> p b d", b=2)
        memT = BIG[:, 28 * 512:32 * 512].rearrange("p (c n) -> p c n", c=8)

        def c_hT(c=None):
            return [f"HY{c}"] if c is not None else [f"HY{i}" for i in range(8)]

        def c_yT(c=None):
            return [f"HY{8 + c}"] if c is not None else [f"HY{8 + i}" for i in range(8)]

        c_oraw = [f"HY{i}" for i in range(16)]

        def c_oraw_b(b):
            return [f"HY{4 * b + i}" for i in range(4)]

        def c_aT(f):
            return [f"B{f}"]

        def c_mf(j):
            return [f"B{2 * j}", f"B{2 * j + 1}"]

        def c_mbf(j):
            return [f"B{16 + j}"]

        def c_xn(b):
            return [f"B{16 + 2 * b}", f"B{17 + 2 * b}"]

        def c_QT(h):
            return [f"B{24 + h}"]

        c_mtf = [f"B{16 + i}" for i in range(8)]
        c_memn = [f"B{24 + i}" for i in range(4)]
        c_memT = [f"B{28 + i}" for i in range(4)]
        BANKC = {0: ["ps0"], 1: ["ps1"], 2: ["ps2"], 3: ["ps3"], 4: ["ps4"],
                 5: ["ps5"], 6: ["ps6"], 7: ["ps7"]}

        sems = {}

        def sem(name):
            if name not in sems:
                sems[name] = es.enter_context(nc.semaphore(name))
            return sems[name]

        for e in Prog.ENGS:
            sem("e_" + e)

        rr = {"ab": 0, "cd": 0, "slot": 0, "qb": 0, "t": 0, "pt": 0, "o": 0, "tp": 0, "alt": 0}

        def bankA():
            rr["ab"] ^= 1
            return rr["ab"]

        def bankB():
            rr["cd"] ^= 1
            return 2 + rr["cd"]

        def alt():
            rr["alt"] ^= 1
            return "act" if rr["alt"] else "dve"

        def dma(queue, out, in_, reads, writes, semname, extra=()):
            sem(semname)
            return E(queue, lambda e, o=out, i=in_: e.dma_start(out=o, in_=i), reads=reads, writes=writes,
                     dma_sem=semname, extra=extra)

        def mm(out, lhsT, rhs, start, stop, reads, writes, sgc=False):
            return E("pe", lambda e: e.matmul(out, lhsT=lhsT, rhs=rhs, start=start, stop=stop, skip_group_check=sgc),
                     reads=reads, writes=writes)

        def tr(out, in_, reads, writes):
            return E("pe", lambda e: e.transpose(out, in_, cmat[:, 0, :]), reads=reads + ["cmat"], writes=writes)

        def act(out, in_, func, reads, writes, scale=1.0, bias=0.0, accum_out=None):
            def fn(e):
                kw = {}
                if accum_out is not None:
                    kw["accum_out"] = accum_out
                return e.activation(out=out, in_=in_, func=func, bias=bias, scale=scale, **kw)
            return E("act", fn, reads=reads, writes=writes)

        def ts(eng, out, in0, s1, s2, op0, op1, reads, writes):
            def fn(e):
                if op1 is None:
                    return e.tensor_scalar(out=out, in0=in0, scalar1=s1, scalar2=None, op0=op0)
                return e.tensor_scalar(out=out, in0=in0, scalar1=s1, scalar2=s2, op0=op0, op1=op1)
            return E(eng, fn, reads=reads, writes=writes)

        def tt(eng, out, in0, in1, op, reads, writes):
            return E(eng, lambda e: e.tensor_tensor(out=out, in0=in0, in1=in1, op=op), reads=reads, writes=writes)

        def stt(out, in0, scalar, in1, op0, op1, reads, writes):
            return E("dve", lambda e: e.scalar_tensor_tensor(out=out, in0=in0, scalar=scalar, in1=in1, op0=op0, op1=op1),
                     reads=reads, writes=writes)

        def copy(eng, out, in_, reads, writes):
            if eng == "act":
                return act(out, in_, AF.Copy, reads, writes)
            return E(eng, lambda e: e.tensor_copy(out=out, in_=in_), reads=reads, writes=writes)

        def memset(eng, ap, val, writes):
            return E(eng, lambda e: e.memset(ap, val), writes=writes)

        def rstd_from(ms_ap, out_ap, cells_in, cells_out):
            act(out_ap, ms_ap, AF.Ln, cells_in, cells_out)
            return act(out_ap, out_ap, AF.Exp, cells_out, cells_out, scale=-0.5)

        conv_tok = {}

        def wsrc(l, t):
            if t >= 46:
                W, r0, c0 = w_sw_d[l], 0, (t - 46) * 512
            elif t < T_KV:
                W, r0, c0 = w_in_d[l], 0, (t - T_IN) * 512
            elif t < T_PA:
                W, r0, c0 = w_kv_d[l], 0, (t - T_KV) * 512
            elif t < T_PL:
                W, r0, c0 = w_pa_d[l], 0, (t - T_PA) * 512
            elif t < T_PM:
                W, r0, c0 = w_pl_d[l], 0, (t - T_PL) * 512
            elif t < T_OUT:
                W, r0, c0 = w_pm_d[l], 0, (t - T_PM) * 512
            elif t < T_UP:
                W, r0, c0 = w_out_d[l], 0, (t - T_OUT) * 512
            elif t < T_DN:
                W, r0, c0 = w_up_d[l], 0, (t - T_UP) * 512
            else:
                i = t - T_DN
                W, r0, c0 = w_dn_d[l], (i % 4) * 1024, (i // 4) * 512
            return W[r0:r0 + 1024, c0:c0 + 512].rearrange("(c p) n -> p c n", p=P)

        use_order = [2, 48, 3, 49, 4, 5, 0, 46, 1, 47, 22, 12, 23, 13, 6, 8, 7, 9, "rg", 24, 14, 25, 15, 18, 19, 20, 21, 10, 11, 26, 16,
                     27, 17, 28, 29] + list(range(30, 46))
        NGRP = 8
        grp_of = {}
        for i, t in enumerate(use_order):
            grp_of[t] = min(i * NGRP // len(use_order), NGRP - 1)

        def emit_conv(l, items):
            for t in items:
                g = grp_of[t]
                sname = f"cv{g}"
                if t == "rg":
                    o = rgsc_d[l].rearrange("p (a n d) -> p a n d", a=2, n=8)
                    i = rg_d[l].rearrange("a n c d -> c a n d")
                else:
                    o = wsc_d[l, t].rearrange("p (c n) -> p c n", c=8)
                    i = wsrc(l, t)
                op = dma("pool", o, i, [], [], sname)
                conv_tok[(l, t)] = op
                conv_last[(l, g)] = op

        conv_last = {}

        def conv_dep(l, t):
            return conv_last[(l, grp_of[t])]

        def load_w(l, t):
            s = rr["slot"]
            rr["slot"] = (s + 1) % NSLOT
            dma("sp", WS[s][:, :, :], wsc_d[l, t].rearrange("p (c n) -> p c n", c=8), [], [f"ws{s}"], f"wl{s}",
                extra=[conv_dep(l, t)])
            return WS[s], [f"ws{s}"]

        cst_ops = []
        cst_ops.append(dma("sp", pv[:, :, :], pvec_d.rearrange("l p v -> p l v"), [], ["pv"], "cst"))
        cst_ops.append(dma("sp", lamv[:, :, :], lamv_d.rearrange("l p v -> p l v"), [], ["xc"], "cst"))
        cst_ops.append(dma("sp", cmat_f[:, :, :], cmat_d.rearrange("k p n -> p k n"), [], ["cmat_f"], "cst"))
        last_cst = cst_ops[-1]
        pg.last_write["pv"] = last_cst
        pg.last_write["xc"] = last_cst
        pg.last_write["cmat_f"] = last_cst

        first_l = layers[0]
        emit_conv(first_l, use_order)
        pending_conv = []
        for l in layers[1:]:
            pending_conv.append((l, list(use_order)))

        copy("act", cmat[:, :, :], cmat_f[:, :, :], ["cmat_f"], ["cmat"])
        memset("dve", ones_bf[:, :], 1.0, ["ones"])
        memset("dve", small[:, 60:64], -0.5, ["nhalf"])
        memset("pool", VC[:, :, :, 128:130], 1.0, [f"VC{b}" for b in range(NB)])

        for l in layers:
            lam_init = 0.8 - 0.6 * math.exp(-0.3 * l)
            dl = der[:, l, :]
            pl_ = pv[:, l, :]
            cd = [f"der{l}"]
            act(dl[:, 0:8], pl_[:, 104:112], AF.Exp, ["pv"], cd, scale=-1.0)
            act(dl[:, 0:8], dl[:, 0:8], AF.Ln, cd, cd, scale=1.0, bias=1.0)
            ts("dve", dl[:, 8:16], dl[:, 0:8], -8.0, None, ALU.mult, None, cd, cd)
            ts("dve", dl[:, 0:8], dl[:, 0:8], -4.0, None, ALU.mult, None, cd, cd)
            ts("dve", dl[:, 16:40], pl_[:, 24:48], 0.5, None, ALU.mult, None, ["pv"] + cd, cd)
            ts("dve", dl[:, 40:56], pl_[:, 88:104], 0.5, None, ALU.mult, None, ["pv"] + cd, cd)
            lv = lamv[:, l, :]
            tt("dve", t1_t[0][:, 0:64], lv[:, 0:64], lv[:, 64:128], ALU.mult, ["xc"], ["t1_0"])
            E("dve", lambda e, o=dl[:, 58:59]: e.reduce_sum(out=o, in_=t1_t[0][:, 0:64], axis=AX.X), reads=["t1_0"], writes=cd)
            tt("dve", t1_t[0][:, 0:64], lv[:, 128:192], lv[:, 192:256], ALU.mult, ["xc"] + cd, ["t1_0"])
            E("dve", lambda e, o=dl[:, 59:60]: e.reduce_sum(out=o, in_=t1_t[0][:, 0:64], axis=AX.X), reads=["t1_0"], writes=cd)
            act(dl[:, 60:62], dl[:, 58:60], AF.Exp, cd, cd)
            tt("dve", dl[:, 56:57], dl[:, 61:62], dl[:, 60:61], ALU.subtract, cd, cd)
            ts("dve", dl[:, 56:57], dl[:, 56:57], -lam_init, None, ALU.add, None, cd, cd)
            ts("dve", dl[:, 57:58], pl_[:, 112:113], 1.0 - lam_init, None, ALU.mult, None, ["pv"] + cd, cd)

        def prenorm(l, gcol0):
            for b in range(4):
                act(junk[:, :], xt[:, b, :], AF.Square, ["xt"], ["ssq"], accum_out=small[:, b:b + 1])
            ts("dve", small[:, 8:12], small[:, 0:4], 1.0 / D, EPS, ALU.mult, ALU.add, ["ssq"], ["ms"])
            rstd_from(small[:, 8:12], small[:, 12:16], ["ms"], ["rstd"])
            for b in range(4):
                ts("dve", xn[:, b, :], xt[:, b, :], small[:, 12 + b:13 + b], None, ALU.mult, None, ["xt", "rstd"], c_xn(b))
            for c in range(8):
                half = 0
                pc = "ps7"
                for b in range(4):
                    tr(pT[:, half * 512 + b * P: half * 512 + (b + 1) * P], xn[:, b, c * P:(c + 1) * P], c_xn(b), [pc])
                act(hT[:, c, :], pT[:, half * 512:(half + 1) * 512], AF.Copy, [pc, "pv"], c_hT(c),
                    scale=pv[:, l, gcol0 + c:gcol0 + c + 1])

        def inproj_chunk(wt, wc, j, bank, N=TT, rhs=None, rcells=None):
            rhs = hT if rhs is None else rhs
            for c in range(8):
                mm(ps[bank][:, 0:N], wt[:, c, j * P:(j + 1) * P], rhs[:, c, :],
                   c == 0, c == 7, wc + (c_hT(c) if rcells is None else rcells), BANKC[bank])

        def rope(bank, dest, dcells, wts, wsc_, j):
            i = rr["qb"]
            rr["qb"] ^= 1
            bB = bankB()
            inproj_chunk(wts, wsc_, j, bB)
            tt("dve", t1_t[i][:, :], ps[bank][:, :], ropet[:, 0, :], ALU.mult, BANKC[bank] + ["ropet"], [f"t1_{i}"])
            tt("dve", t2_t[i][:, :], ps[bB][:, :], ropet[:, 1, :], ALU.mult, BANKC[bB] + ["ropet"], [f"t2_{i}"])
            tt("pool", dest, t1_t[i][:, :], t2_t[i][:, :], ALU.add, [f"t1_{i}", f"t2_{i}"], dcells)

        def attention(l, t):
            dl = der[:, l, :]
            for h in range(NH):
                nkb = 4 * (t + 1)
                for kb in range(nkb):
                    j = kb - 4 * t
                    q0 = P * j if j > 0 else 0
                    pi = rr["pt"]
                    rr["pt"] = (pi + 1) % 2
                    sb_ = pi * 2
                    kc = [f"KT{h}_{kb // 4}"]
                    mm(ps[sb_][:, q0:TT], KT[0:64, h, kb * P:(kb + 1) * P], QT[0:64, h, q0:TT], True, True,
                       kc + c_QT(h), BANKC[sb_])
                    mm(ps[sb_ + 1][:, q0:TT], KT[64:128, h, kb * P:(kb + 1) * P], QT[64:128, h, q0:TT], True, True,
                       kc + c_QT(h), BANKC[sb_ + 1])
                    for m in range(2):
                        act(PT[pi][m][:, q0:TT], ps[sb_ + m][:, q0:TT], AF.Exp, BANKC[sb_ + m], [f"pt{pi}_{m}"], scale=0.125)
                        if j >= 0:
                            memset("pool", PT[pi][m][64:128, q0:q0 + 64], 0.0, [f"pt{pi}_{m}"])
                    for qb in range(max(j, 0), 4):
                        for m in range(2):
                            oi = qb * 2 + m
                            bank = 4 + oi // 3
                            off = (oi % 3) * 130
                            mm(ps[bank][:, off:off + 130], PT[pi][m][:, qb * P:(qb + 1) * P], VC[:, kb, h, :],
                               (kb == 0 and oi % 3 == 0), (kb == 4 * t + qb), [f"pt{pi}_{m}", f"VC{kb}"], BANKC[bank], sgc=True)
                for qb in range(4):
                    if True:
                        oi1, oi2 = qb * 2, qb * 2 + 1
                        O1 = ps[4 + oi1 // 3][:, (oi1 % 3) * 130:(oi1 % 3) * 130 + 130]
                        O2 = ps[4 + oi2 // 3][:, (oi2 % 3) * 130:(oi2 % 3) * 130 + 130]
                        k = rr["o"]
                        rr["o"] ^= 1
                        st = small[:, 16 + 8 * k:24 + 8 * k]
                        sc = [f"ast{k}"]
                        E("dve", lambda e, o=st[:, 0:1], i=O1[:, 128:129]: e.reciprocal(out=o, in_=i), reads=BANKC[4 + oi1 // 3], writes=sc)
                        E("dve", lambda e, o=st[:, 1:2], i=O2[:, 128:129]: e.reciprocal(out=o, in_=i), reads=BANKC[4 + oi2 // 3], writes=sc)
                        tt("dve", st[:, 2:3], st[:, 1:2], dl[:, 56:57], ALU.mult, sc + [f"der{l}"], sc)
                        ts("dve", tt_t[k][:, :], O2[:, 0:P], st[:, 2:3], None, ALU.mult, None, BANKC[4 + oi2 // 3] + sc, [f"tt{k}"])
                        stt(o_t[k][:, :], O1[:, 0:P], st[:, 0:1], tt_t[k][:, :], ALU.mult, ALU.add, BANKC[4 + oi1 // 3] + [f"tt{k}"] + sc, [f"o{k}"])
                        act(junk[:, 0:P], o_t[k][:, :], AF.Square, [f"o{k}"], sc, accum_out=st[:, 3:4])
                        ts("dve", st[:, 4:5], st[:, 3:4], 1.0 / P, EPS, ALU.mult, ALU.add, sc, sc)
                        rstd_from(st[:, 4:5], st[:, 5:6], sc, sc)
                        ts("dve", ybf[:, qb, :], o_t[k][:, :], st[:, 5:6], None, ALU.mult, None, [f"o{k}"] + sc, [f"ybf{qb}"])
                half = 0
                pc = "ps7"
                for qb in range(4):
                    tr(pT[:, half * 512 + qb * P: half * 512 + (qb + 1) * P], ybf[:, qb, :], [f"ybf{qb}"], [pc])
                ts("dve", yT[:, h, :], pT[:, half * 512:(half + 1) * 512], dl[:, 57:58], None, ALU.mult, None,
                   [pc, f"der{l}"], c_yT(h))

        def proj_gate(l, bidx, wp, first, last):
            dl = der[:, l, :]
            for hh in range(2):
                wtp, wpc = load_w(l, wp + hh)
                wtg, wgc = load_w(l, 12 + 2 * bidx + hh)
                for j in range(4):
                    jc = hh * 4 + j
                    bA = bankA()
                    bB = bankB()
                    for c in range(8):
                        mm(ps[bA][:, :], wtp[:, c, j * P:(j + 1) * P], yT[:, c, :], c == 0, c == 7, wpc + c_yT(c), BANKC[bA])
                    for c in range(8):
                        mm(ps[bB][:, :], wtg[:, c, j * P:(j + 1) * P], hT[:, c, :], c == 0, c == 7, wgc + c_hT(c), BANKC[bB])
                    i = rr["t"]
                    rr["t"] ^= 1
                    col = 16 + bidx * 8 + jc
                    act(t1_t[i][:, :], ps[bB][:, :], AF.Tanh, BANKC[bB] + [f"der{l}"], [f"t1_{i}"], scale=0.5, bias=dl[:, col:col + 1])
                    if first:
                        stt(m_f[:, jc, :], t1_t[i][:, :], 1.0, ps[bA][:, :], ALU.add, ALU.mult, [f"t1_{i}"] + BANKC[bA], c_mf(jc))
                    else:
                        stt(t2_t[i][:, :], t1_t[i][:, :], 1.0, ps[bA][:, :], ALU.add, ALU.mult, [f"t1_{i}"] + BANKC[bA], [f"t2_{i}"])
                        if last:
                            tt("dve", m_bf[:, jc, :], m_f[:, jc, :], t2_t[i][:, :], ALU.add, c_mf(jc) + [f"t2_{i}"], c_mbf(jc))
                        else:
                            tt("dve", m_f[:, jc, :], m_f[:, jc, :], t2_t[i][:, :], ALU.add, c_mf(jc) + [f"t2_{i}"], c_mf(jc))

        def lru(l, t):
            dl = der[:, l, :]
            pl_ = pv[:, l, :]
            dma("sp", rgw[:, :, :, :], rgsc_d[l].rearrange("p (a n d) -> p a n d", a=2, n=8), [], ["rgw"], "rgl",
                extra=[conv_dep(l, "rg")])
            wts = {}
            for hh in range(2):
                wts[("x", hh)] = load_w(l, 6 + hh)
                wts[("y", hh)] = load_w(l, 8 + hh)
                for j in range(4):
                    c = hh * 4 + j
                    wt, wc = wts[("x", hh)]
                    bA = bankA()
                    inproj_chunk(wt, wc, j, bA)
                    if t == 0:
                        memset("pool", zxe[:, 0:3], 0.0, ["zxe"])
                    else:
                        copy("pool", zxe[:, 0:3], cz[:, c, :], [f"cz{c}"], ["zxe"])
                    act(zxe[:, 3:TT + 3], ps[bA][:, :], AF.Copy, BANKC[bA], ["zxe"])
                    copy("pool", cz[:, c, :], zxe[:, TT:TT + 3], ["zxe"], [f"cz{c}"])
                    ts("dve", xc[:, :], zxe[:, 3:TT + 3], pl_[:, 48 + 24 + c:48 + 25 + c], pl_[:, 80 + c:81 + c], ALU.mult, ALU.add,
                       ["zxe", "pv"], ["xc"])
                    for jj in (2, 1, 0):
                        stt(xc[:, :], zxe[:, jj:TT + jj], pl_[:, 48 + 8 * jj + c:48 + 8 * jj + c + 1], xc[:, :], ALU.mult, ALU.add,
                            ["zxe", "xc", "pv"], ["xc"])
                    copy("pool", xcb[:, :], xc[:, :], ["xc"], ["xcb"])
                    bB = bankB()
                    mm(ps[bB][:, :], rgw[:, 0, c, :], xcb[:, :], True, True, ["rgw", "xcb"], BANKC[bB])
                    bC = bankB()
                    mm(ps[bC][:, :], rgw[:, 1, c, :], xcb[:, :], True, True, ["rgw", "xcb"], BANKC[bC])
                    act(thr[:, :], ps[bB][:, :], AF.Tanh, BANKC[bB] + [f"der{l}"], ["t1_0"], scale=0.5, bias=dl[:, 40 + c:41 + c])
                    act(thi[:, :], ps[bC][:, :], AF.Tanh, BANKC[bC] + [f"der{l}"], ["t1_1"], scale=0.5, bias=dl[:, 48 + c:49 + c])
                    act(a_t[:, :], thr[:, :], AF.Exp, ["t1_0", f"der{l}"], ["t2_0"], scale=dl[:, c:c + 1], bias=dl[:, c:c + 1])
                    act(a2_t[:, :], thr[:, :], AF.Exp, ["t1_0", f"der{l}"], ["t2_1"], scale=dl[:, 8 + c:9 + c], bias=dl[:, 8 + c:9 + c])
                    ts("dve", a2_t[:, :], a2_t[:, :], -1.0, 1.0, ALU.mult, ALU.add, ["t2_1"], ["t2_1"])
                    act(a2_t[:, :], a2_t[:, :], AF.Sqrt, ["t2_1"], ["t2_1"], scale=0.25)
                    stt(thi[:, :], thi[:, :], 1.0, xc[:, :], ALU.add, ALU.mult, ["t1_1", "xc"], ["t1_1"])
                    tt("pool", thi[:, :], thi[:, :], a2_t[:, :], ALU.mult, ["t1_1", "t2_1"], ["t1_1"])
                    init = 0.0 if t == 0 else hcar[:, c:c + 1]
                    E("dve", lambda e, i_=init: e.tensor_tensor_scan(out=hs_t[:, :], data0=a_t[:, :], data1=thi[:, :], initial=i_,
                                                                    op0=ALU.mult, op1=ALU.add),
                      reads=["t2_0", "t1_1", f"hcar{c}"], writes=["hs_t"])
                    copy("pool", hcar[:, c:c + 1], hs_t[:, TT - 1:TT], ["hs_t"], [f"hcar{c}"])
                    wt, wc = wts[("y", hh)]
                    bA = bankA()
                    inproj_chunk(wt, wc, j, bA)
                    act(gz_t[:, :], ps[bA][:, :], AF.Gelu_apprx_tanh, BANKC[bA], ["gz_t"])
                    tt("dve", yT[:, c, :], hs_t[:, :], gz_t[:, :], ALU.mult, ["hs_t", "gz_t"], c_yT(c))

        def mem_kv(l, s):
            dma("sp", mt_f[:, :, :], mem_d[s * MEM:(s + 1) * MEM, :].rearrange("(b p) d -> p b d", p=P), [], c_mtf, "meml")
            for b in range(2):
                act(junk[:, :], mt_f[:, b, :], AF.Square, c_mtf, ["ssqm"], accum_out=small[:, 32 + b:33 + b])
            ts("dve", small[:, 34:36], small[:, 32:34], 1.0 / D, EPS, ALU.mult, ALU.add, ["ssqm"], ["msm"])
            rstd_from(small[:, 34:36], small[:, 36:38], ["msm"], ["rstdm"])
            for b in range(2):
                ts("dve", memn[:, b, :], mt_f[:, b, :], small[:, 36 + b:37 + b], None, ALU.mult, None, c_mtf + ["rstdm"], c_memn)
            for c in range(8):
                half = 0
                pc = "ps7"
                for b in range(2):
                    tr(pT[:, half * 512 + b * P: half * 512 + (b + 1) * P], memn[:, b, c * P:(c + 1) * P], c_memn, [pc])
                act(memT[:, c, :], pT[:, half * 512:half * 512 + MEM], AF.Copy, [pc, "pv"], c_memT,
                    scale=pv[:, l, 16 + c:17 + c])
            for hh in range(2):
                wt, wc = load_w(l, T_KV + hh)
                for j in range(4):
                    jc = hh * 4 + j
                    bA = bankA()
                    inproj_chunk(wt, wc, j, bA, N=MEM, rhs=memT, rcells=c_memT)
                    copy(alt(), KmT[:, jc, :], ps[bA][:, 0:MEM], BANKC[bA], ["KmT"])
            for hh in range(2):
                wt, wc = load_w(l, T_KV + 2 + hh)
                for b in range(2):
                    bA = bankA()
                    for c in range(8):
                        mm(ps[bA][:, :], memT[:, c, b * P:(b + 1) * P], wt[:, c, :], c == 0, c == 7, wc + c_memT, BANKC[bA])
                    copy(alt(), Vm[:, b, hh * 512:(hh + 1) * 512], ps[bA][:, :], BANKC[bA], ["Vm"])

        def mem_attn(l):
            for hh in range(2):
                wt, wc = load_w(l, 10 + hh)
                for j in range(4):
                    jc = hh * 4 + j
                    bA = bankA()
                    inproj_chunk(wt, wc, j, bA)
                    copy(alt(), QT[:, jc, :], ps[bA][:, :], BANKC[bA], c_QT(jc))
            for g in range(4):
                pis = []
                for mb in range(2):
                    bk = mb
                    for dc in range(2):
                        mm(ps[bk][:, :], KmT[:, 2 * g + dc, mb * P:(mb + 1) * P], QT[:, 2 * g + dc, :], dc == 0, dc == 1,
                           ["KmT"] + c_QT(2 * g + dc), BANKC[bk])
                    act(PT[mb][0][:, :], ps[bk][:, :], AF.Exp, BANKC[bk], [f"pt{mb}_0"], scale=1.0 / 16.0)
                for mb in range(2):
                    mm(ps[2][:, :], ones_bf[:, :], PT[mb][0][:, :], mb == 0, mb == 1, ["ones", f"pt{mb}_0"], BANKC[2])
                E("dve", lambda e: e.reciprocal(out=t1_t[0][:, :], in_=ps[2][:, :]), reads=BANKC[2], writes=["t1_0"])
                for vc in range(2):
                    bk = 4 + vc
                    for mb in range(2):
                        mm(ps[bk][:, :], Vm[:, mb, g * 256 + vc * P: g * 256 + (vc + 1) * P], PT[mb][0][:, :], mb == 0, mb == 1,
                           ["Vm", f"pt{mb}_0"], BANKC[bk])
                    tt("dve", yT[:, 2 * g + vc, :], ps[bk][:, :], t1_t[0][:, :], ALU.mult, BANKC[bk] + ["t1_0"], c_yT(2 * g + vc))

        def post(l, gidx, eps, kind):
            dma("sp", gp[:, :], gpost_d[l, gidx], [], ["hs_t", "gz_t"], "gpl")
            for half in range(2):
                banks = [0, 1, 2, 3] if half == 0 else [4, 5, 6, 0]
                if kind == "out":
                    wt, wc = load_w(l, T_OUT + half)
                    for b in range(4):
                        for c in range(8):
                            mm(ps[banks[b]][:, :], m_bf[:, c, b * P:(b + 1) * P], wt[:, c, :], c == 0, c == 7,
                               wc + c_mbf(c), BANKC[banks[b]])
                else:
                    for kq in range(4):
                        wt, wc = load_w(l, T_DN + half * 4 + kq)
                        for b in range(4):
                            for c in range(8):
                                f = kq * 8 + c
                                mm(ps[banks[b]][:, :], aT[:, f, b * P:(b + 1) * P], wt[:, c, :], (kq == 0 and c == 0),
                                   (kq == 3 and c == 7), wc + c_aT(f), BANKC[banks[b]])
                for b in range(4):
                    copy("dve", o_raw[:, b, half * 512:(half + 1) * 512], ps[banks[b]][:, :], BANKC[banks[b]], c_oraw_b(b))
                    act(junk[:, 0:512], o_raw[:, b, half * 512:(half + 1) * 512], AF.Square, c_oraw_b(b), ["pssq"],
                        accum_out=small[:, 40 + half * 4 + b:41 + half * 4 + b])
            tt("dve", small[:, 48:52], small[:, 40:44], small[:, 44:48], ALU.add, ["pssq"], ["pms"])
            ts("dve", small[:, 48:52], small[:, 48:52], 1.0 / D, eps, ALU.mult, ALU.add, ["pms"], ["pms"])
            rstd_from(small[:, 48:52], small[:, 52:56], ["pms"], ["prstd"])
            for b in range(4):
                stt(o_raw[:, b, :], o_raw[:, b, :], small[:, 52 + b:53 + b], gp[:, :], ALU.mult, ALU.mult,
                    c_oraw_b(b) + ["prstd", "hs_t", "gz_t"], c_oraw_b(b))
                tt("dve", xt[:, b, :], xt[:, b, :], o_raw[:, b, :], ALU.add, ["xt"] + c_oraw_b(b), ["xt"])

        def mlp_up(l):
            for ti in range(8):
                wt, wc = load_w(l, T_UP + ti)
                for j in range(4):
                    f = ti * 4 + j
                    bA = bankA()
                    inproj_chunk(wt, wc, j, bA)
                    i = rr["t"]
                    rr["t"] ^= 1
                    act(t1_t[i][:, :], ps[bA][:, :], AF.Relu, BANKC[bA], [f"t1_{i}"])
                    tt("dve", aT[:, f, :], t1_t[i][:, :], t1_t[i][:, :], ALU.mult, [f"t1_{i}"], c_aT(f))

        for s in range(nseq):
            for li, l in enumerate(layers):
                src = x_d if li == 0 else xmid_d
                dst = out_d if li == len(layers) - 1 else xmid_d
                for t in range(NT):
                    if pending_conv and s == 0 and li == 0:
                        pl, items = pending_conv[0]
                        n = (len(items) + (NT - t) - 1) // (NT - t)
                        emit_conv(pl, items[:n])
                        del items[:n]
                        if not items:
                            pending_conv.pop(0)
                    r0 = s * S + t * TT
                    dcell = f"xm{s}_{t}"
                    dma("sp", xt[:, :, :], src[r0:r0 + TT, :].rearrange("(b p) d -> p b d", p=P),
                        [dcell] if li > 0 else [], ["xt"], "xld")
                    dma("sp", ropet[:, :, :], rope_d[:, :, t * TT:(t + 1) * TT].rearrange("k p s -> p k s"), [], ["ropet"], "rpl")
                    import os
                    DS = int(os.environ.get("DEBUG_STOP", "99"))
                    if DS >= 1:
                        prenorm(l, 0)
                    for hh in range(2):
                        if DS < 2:
                            break
                        wt, wc = load_w(l, 2 + hh)
                        wts, wsc_ = load_w(l, 48 + hh)
                        for j in range(4):
                            h = hh * 4 + j
                            bA = bankA()
                            inproj_chunk(wt, wc, j, bA)
                            rope(bA, KT[:, h, t * TT:(t + 1) * TT], [f"KT{h}_{t}"], wts, wsc_, j)
                    for hh in range(2):
                        if DS < 3:
                            break
                        wt, wc = load_w(l, 4 + hh)
                        for b in range(4):
                            bA = bankA()
                            for c in range(8):
                                mm(ps[bA][:, :], hT[:, c, b * P:(b + 1) * P], wt[:, c, :], c == 0, c == 7, wc + c_hT(c), BANKC[bA])
                            copy(alt(), VC[:, t * 4 + b, hh * 4:(hh + 1) * 4, 0:P],
                                 ps[bA][:, :].rearrange("p (h v) -> p h v", h=4), BANKC[bA], [f"VC{t * 4 + b}"])
                    for hh in range(2):
                        if DS < 4:
                            break
                        wt, wc = load_w(l, 0 + hh)
                        wts, wsc_ = load_w(l, 46 + hh)
                        for j in range(4):
                            h = hh * 4 + j
                            bA = bankA()
                            inproj_chunk(wt, wc, j, bA)
                            rope(bA, QT[:, h, :], c_QT(h), wts, wsc_, j)
                    if DS >= 5:
                        attention(l, t)
                    if DS >= 6:
                        proj_gate(l, 0, T_PA, True, False)
                    if DS >= 7:
                        lru(l, t)
                    if DS >= 8:
                        proj_gate(l, 1, T_PL, False, False)
                    if DS >= 9:
                        if t == 0:
                            mem_kv(l, s)
                        mem_attn(l)
                    if DS >= 10:
                        proj_gate(l, 2, T_PM, False, True)
                    if DS >= 11:
                        post(l, 0, 4.0 * EPS, "out")
                    if DS >= 12:
                        prenorm(l, 8)
                        mlp_up(l)
                    if DS >= 13:
                        post(l, 1, EPS, "dn")
                    dma("sp", dst[r0:r0 + TT, :].rearrange("(b p) d -> p b d", p=P), xt[:, :, :], ["xt"],
                        [dcell] if li < len(layers) - 1 else ["outc"], "xst", extra=[pg.last_write["xst_prev"]] if "xst_prev" in pg.last_write else [])
                    pg.last_write["xst_prev"] = pg.ops[-1]

        final_store = pg.ops[-1]
        dma_cnt = pg.finalize()

        engmap = {"pe": "tensor", "act": "scalar", "dve": "vector", "pool": "gpsimd", "sp": "sync"}
        with nc.Block() as block:
            def make(engname):
                myops = [op for op in pg.ops if op.eng == engname]

                def body(e):
                    for op in myops:
                        for d in op.waits:
                            if d.dma_sem is not None:
                                e.wait_ge(sems[d.dma_sem], d.val)
                            else:
                                e.wait_ge(sems["e_" + d.eng], d.val)
                        ins = op.fn(e)
                        if op.dma_sem is not None:
                            ins.then_inc(sems[op.dma_sem], 16)
                        elif op.signal:
                            ins.then_inc(sems["e_" + op.eng], 1)
                    if engname == "sp":
                        e.wait_ge(sems["xst"], 16 * dma_cnt["xst"])
                return body

            for en in Prog.ENGS:
                getattr(block, engmap[en])(make(en))
    return nc


def _pack_small(inputs, L, S):
    f = np.float32

    def fm(v):
        return np.asarray(v, f).reshape(8, P).T

    pvec = np.zeros((L, P, NV), f)
    lamv = np.zeros((L, P, 256), f)
    gpost = np.zeros((L, 2, P, D), f)
    for l in range(L):
        pvec[l, :, 0:8] = fm(inputs["norm_mix_pre"][l])
        pvec[l, :, 8:16] = fm(inputs["norm_mlp_pre"][l])
        pvec[l, :, 16:24] = fm(inputs["norm_mem"][l])
        for b in range(3):
            pvec[l, :, 24 + 8 * b:32 + 8 * b] = fm(inputs["b_gate"][l][b])
        for j in range(4):
            pvec[l, :, 48 + 8 * j:56 + 8 * j] = fm(inputs["conv_w"][l][j])
        pvec[l, :, 80:88] = fm(inputs["conv_b"][l])
        pvec[l, :, 88:96] = fm(inputs["rg_ba"][l])
        pvec[l, :, 96:104] = fm(inputs["rg_bi"][l])
        pvec[l, :, 104:112] = fm(inputs["rg_lambda"][l])
        pvec[l, :, 112] = np.asarray(inputs["diff_subln"][l], f)
        for i, k in enumerate(["lambda_q1", "lambda_k1", "lambda_q2", "lambda_k2"]):
            lamv[l, :, 64 * i:64 * (i + 1)] = np.asarray(inputs[k][l], f)[None, :]
        gpost[l, 0] = np.asarray(inputs["norm_mix_post"][l], f)[None, :]
        gpost[l, 1] = np.asarray(inputs["norm_mlp_post"][l], f)[None, :]
    half = 32
    pos = np.arange(S, dtype=f)
    inv = (1.0 / (f(10000.0) ** (np.arange(half, dtype=f) / f(half)))).astype(f)
    ang = (pos[:, None] * inv[None, :]).astype(f)
    cos = np.cos(ang).astype(f).T
    sin = np.sin(ang).astype(f).T
    rope = np.zeros((2, P, S), f)
    for p in range(P):
        rope[0, p] = cos[p % 32]
        rope[1, p] = -sin[p % 32] if (p % 64) < 32 else sin[p % 32]
    cmat = np.zeros((2, P, P), f)
    cmat[0] = np.eye(P, dtype=f)
    for m in range(P):
        k = m + 32 if (m % 64) < 32 else m - 32
        cmat[1, k, m] = 1.0
    rg = np.stack([np.asarray(inputs["rg_wa"], f), np.asarray(inputs["rg_wi"], f)], axis=1)
    return pvec, lamv, gpost, rope, cmat, rg


def run(inputs, ncores, nseq, S, layer_groups, L):
    pvec, lamv, gpost, rope, cmat, rg = _pack_small(inputs, L, S)
    f = np.float32
    x = np.ascontiguousarray(np.asarray(inputs["x"], f))
    mem = np.ascontiguousarray(np.asarray(inputs["mem"], f))
    cols = np.arange(2048)
    partner = np.where((cols % 64) < 32, cols + 32, cols - 32)
    w_sw = np.ascontiguousarray(np.asarray(inputs["w_in"], f)[:, :, partner])
    shared = {
        "w_sw": w_sw,
        "w_in": np.ascontiguousarray(inputs["w_in"], f), "w_kv": np.ascontiguousarray(inputs["w_kv_mem"], f),
        "w_pa": np.ascontiguousarray(inputs["w_proj_attn"], f), "w_pl": np.ascontiguousarray(inputs["w_proj_lru"], f),
        "w_pm": np.ascontiguousarray(inputs["w_proj_mem"], f), "w_out": np.ascontiguousarray(inputs["w_out"], f),
        "w_up": np.ascontiguousarray(inputs["w_mlp_up"], f), "w_dn": np.ascontiguousarray(inputs["w_mlp_down"], f),
        "rg_w": np.ascontiguousarray(rg), "pvec": pvec, "gpost": gpost, "lamv": lamv, "rope": rope, "cmat": cmat,
    }
    cur = x
    for layers in layer_groups:
        nc = build(nseq, S, layers, L)
        in_maps = []
        for c in range(ncores):
            m = dict(shared)
            m["x"] = np.ascontiguousarray(cur[c * nseq:(c + 1) * nseq].reshape(nseq * S, D))
            m["mem"] = np.ascontiguousarray(mem[c * nseq:(c + 1) * nseq].reshape(nseq * MEM, D))
            in_maps.append(m)
        res = run_bass_kernel_spmd(nc, in_maps, core_ids=list(range(ncores)))
        cur = np.concatenate([np.asarray(r["out"]).reshape(nseq, S, D) for r in res.results], axis=0)
    return cur.astype(np.float32)


def kernel(**inputs):
    return run(inputs, 8, 4, 2048, [[0, 1]], 2)
```

```python
import math
import numpy as np
import concourse.bass as bass
import concourse.mybir as mybir
from concourse.bass_utils import run_bass_kernel_spmd

F32 = mybir.dt.float32
BF16 = mybir.dt.bfloat16
AF = mybir.ActivationFunctionType
ALU = mybir.AluOpType
AX = mybir.AxisListType

P = 128
D = 1024
TT = 512
NH = 8
MEM = 256
NIN = 9216
DFF = 4096
EPS = 1e-6
NSLOT = 3
NV = 116
NTILES = 50
SAME_ENGINE_SYNC = True

T_IN, T_KV, T_PA, T_PL, T_PM, T_OUT, T_UP, T_DN = 0, 18, 22, 24, 26, 28, 30, 38


class Op:
    __slots__ = ("eng", "fn", "deps", "dma_sem", "signal", "val", "idx", "waits", "extra")

    def __init__(self, eng, fn, deps, dma_sem=None):
        self.eng = eng
        self.fn = fn
        self.deps = deps
        self.dma_sem = dma_sem
        self.signal = False
        self.val = 0
        self.idx = 0
        self.waits = []
        self.extra = []


class Prog:
    ENGS = ("pe", "act", "dve", "pool", "sp")

    def __init__(self):
        self.ops = []
        self.last_write = {}
        self.readers = {}

    def emit(self, eng, fn, reads=(), writes=(), dma_sem=None, extra=()):
        deps = []
        for c in reads:
            w = self.last_write.get(c)
            if w is not None:
                deps.append(w)
        for c in writes:
            w = self.last_write.get(c)
            if w is not None:
                deps.append(w)
            deps.extend(self.readers.get(c, ()))
        deps.extend(extra)
        op = Op(eng, fn, deps, dma_sem)
        self.ops.append(op)
        for c in writes:
            self.last_write[c] = op
            self.readers[c] = []
        for c in reads:
            self.readers.setdefault(c, []).append(op)
        return op

    def finalize(self):
        cnt = {e: 0 for e in self.ENGS}
        for op in self.ops:
            cnt[op.eng] += 1
            op.idx = cnt[op.eng]
        waited = {e: {} for e in self.ENGS}
        dma_cnt = {}
        for op in self.ops:
            if op.dma_sem is not None:
                dma_cnt[op.dma_sem] = dma_cnt.get(op.dma_sem, 0) + 1
                op.val = 16 * dma_cnt[op.dma_sem]
        for op in self.ops:
            need = {}
            for d in op.deps:
                if d.dma_sem is not None:
                    key = ("dma", d.dma_sem)
                    v = d.val
                else:
                    if d.eng == op.eng and (d.eng == "pe" or d.eng == "sp" or not SAME_ENGINE_SYNC):
                        continue
                    key = ("eng", d.eng)
                    v = d.idx
                if need.get(key, (0, None))[0] < v:
                    need[key] = (v, d)
            for key, (v, d) in need.items():
                if waited[op.eng].get(key, 0) >= v:
                    continue
                waited[op.eng][key] = v
                if key[0] == "eng":
                    d.signal = True
                op.waits.append(d)
        sig = {e: 0 for e in self.ENGS}
        for op in self.ops:
            if op.dma_sem is None and op.signal:
                sig[op.eng] += 1
                op.val = sig[op.eng]
        return dma_cnt


def build(nseq, S, layers, nlayers_total):
    NT = S // TT
    NB = S // P
    nc = bass.Bass("TRN2", target_bir_lowering=False)
    L = nlayers_total

    def din(name, shape, dt=F32):
        return nc.dram_tensor(name, list(shape), dt, kind="ExternalInput").ap()

    x_d = din("x", [nseq * S, D])
    mem_d = din("mem", [nseq * MEM, D])
    w_in_d = din("w_in", [L, D, NIN])
    w_kv_d = din("w_kv", [L, D, 2 * D])
    w_sw_d = din("w_sw", [L, D, 2 * D])
    w_pa_d = din("w_pa", [L, D, D])
    w_pl_d = din("w_pl", [L, D, D])
    w_pm_d = din("w_pm", [L, D, D])
    w_out_d = din("w_out", [L, D, D])
    w_up_d = din("w_up", [L, D, DFF])
    w_dn_d = din("w_dn", [L, DFF, D])
    rg_d = din("rg_w", [L, 2, 8, P, P])
    pvec_d = din("pvec", [L, P, NV])
    gpost_d = din("gpost", [L, 2, P, D])
    lamv_d = din("lamv", [L, P, 4 * 64])
    rope_d = din("rope", [2, P, S])
    cmat_d = din("cmat", [2, P, P])
    out_d = nc.dram_tensor("out", [nseq * S, D], F32, kind="ExternalOutput").ap()
    xmid_d = nc.dram_tensor("xmid", [nseq * S, D], F32, kind="Internal").ap()
    wsc_d = nc.dram_tensor("wsc", [L, NTILES, P, 8 * 512], BF16, kind="Internal").ap()
    rgsc_d = nc.dram_tensor("rgsc", [L, P, 2 * 8 * P], BF16, kind="Internal").ap()

    pg = Prog()
    E = pg.emit

    from contextlib import ExitStack
    es = ExitStack()

    def sb(name, shape, dt):
        return es.enter_context(nc.sbuf_tensor("sb_" + name, list(shape), dt))

    with es:
        KT = sb("KT", [P, NH, S], BF16)
        VC = sb("VC", [P, NB, NH, 130], BF16)
        xt = sb("xt", [P, 4, D], F32)
        HY = sb("HY", [P, 16 * 512], BF16)
        BIG = sb("BIG", [P, 32 * 512], BF16)
        WS = [sb(f"ws{i}", [P, 8, 512], BF16) for i in range(NSLOT)]
        KmT = sb("KmT", [P, 8, MEM], BF16)
        Vm = sb("Vm", [P, 2, D], BF16)
        ropet = sb("ropet", [P, 2, TT], F32)
        rgw = sb("rgw", [P, 2, 8, P], BF16)
        pv = sb("pv", [P, L, NV], F32)
        cmat_f = sb("cmat_f", [P, 2, P], F32)
        cmat = sb("cmat", [P, 2, P], BF16)
        ones_bf = sb("ones_bf", [P, P], BF16)
        der = sb("der", [P, L, 64], F32)
        small = sb("small", [P, 64], F32)
        junk = sb("junk", [P, D], BF16)
        qb_t = [sb(f"qb{i}", [P, TT], BF16) for i in range(2)]
        t1_t = [sb(f"t1_{i}", [P, TT], F32) for i in range(2)]
        t2_t = [sb(f"t2_{i}", [P, TT], F32) for i in range(2)]
        PT = [[sb(f"pt{i}_{m}", [P, TT], BF16) for m in range(2)] for i in range(2)]
        o_t = [sb(f"o_{i}", [P, P], F32) for i in range(2)]
        tt_t = [sb(f"tt_{i}", [P, P], F32) for i in range(2)]
        ybf = sb("ybf", [P, 4, P], BF16)
        zxe = sb("zxe", [P, TT + 3], F32)
        xc = sb("xc", [P, TT], F32)
        xcb = sb("xcb", [P, TT], BF16)
        thr, thi, a_t, a2_t = t1_t[0], t1_t[1], t2_t[0], t2_t[1]
        HG = sb("HG", [P, 2 * TT], F32)
        hs_t = HG[:, 0:TT]
        gz_t = HG[:, TT:2 * TT]
        gp = HG
        cz = sb("cz", [P, 8, 3], F32)
        hcar = sb("hcar", [P, 8], F32)
        assert L == 2
        lamv = xc[:, :].rearrange("p (l v) -> p l v", l=L)

        ps = [es.enter_context(nc.psum_tensor(f"psum{i}", [P, 512], F32)) for i in range(7)]
        pT = es.enter_context(nc.psum_tensor("psum7", [P, 1024], BF16))

        hT = HY[:, 0:8 * 512].rearrange("p (c n) -> p c n", c=8)
        yT = HY[:, 8 * 512:16 * 512].rearrange("p (c n) -> p c n", c=8)
        o_raw = HY[:, :].bitcast(F32).rearrange("p (b d) -> p b d", b=4)
        aT = BIG[:, :].rearrange("p (f n) -> p f n", f=32)
        m_f = BIG[:, 0:16 * 512].bitcast(F32).rearrange("p (c n) -> p c n", c=8)
        m_bf = BIG[:, 16 * 512:24 * 512].rearrange("p (c n) -> p c n", c=8)
        xn = BIG[:, 16 * 512:24 * 512].rearrange("p (b d) -> p b d", b=4)
        mt_f = BIG[:, 16 * 512:24 * 512].bitcast(F32).rearrange("p (b d) -> p b d", b=2)
        QT = BIG[:, 24 * 512:32 * 512].rearrange("p (c n) -> p c n", c=8)
        memn = BIG[:, 24 * 512:28 * 512].rearrange("p (b d) -> p b d", b=2)
        memT = BIG[:, 28 * 512:32 * 512].rearrange("p (c n) -> p c n", c=8)

        def c_hT(c=None):
            return [f"HY{c}"] if c is not None else [f"HY{i}" for i in range(8)]

        def c_yT(c=None):
            return [f"HY{8 + c}"] if c is not None else [f"HY{8 + i}" for i in range(8)]

        c_oraw = [f"HY{i}" for i in range(16)]

        def c_oraw_b(b):
            return [f"HY{4 * b + i}" for i in range(4)]

        def c_aT(f):
            return [f"B{f}"]

        def c_mf(j):
            return [f"B{2 * j}", f"B{2 * j + 1}"]

        def c_mbf(j):
            return [f"B{16 + j}"]

        def c_xn(b):
            return [f"B{16 + 2 * b}", f"B{17 + 2 * b}"]

        def c_QT(h):
            return [f"B{24 + h}"]

        c_mtf = [f"B{16 + i}" for i in range(8)]
        c_memn = [f"B{24 + i}" for i in range(4)]
        c_memT = [f"B{28 + i}" for i in range(4)]
        BANKC = {0: ["ps0"], 1: ["ps1"], 2: ["ps2"], 3: ["ps3"], 4: ["ps4"],
                 5: ["ps5"], 6: ["ps6"], 7: ["ps7"]}

        sems = {}

        def sem(name):
            if name not in sems:
                sems[name] = es.enter_context(nc.semaphore(name))
            return sems[name]

        for e in Prog.ENGS:
            sem("e_" + e)

        rr = {"ab": 0, "cd": 0, "slot": 0, "qb": 0, "t": 0, "pt": 0, "o": 0, "tp": 0, "alt": 0}

        def bankA():
            rr["ab"] ^= 1
            return rr["ab"]

        def bankB():
            rr["cd"] ^= 1
            return 2 + rr["cd"]

        def alt():
            rr["alt"] ^= 1
            return "act" if rr["alt"] else "dve"

        def dma(queue, out, in_, reads, writes, semname, extra=()):
            sem(semname)
            return E(queue, lambda e, o=out, i=in_: e.dma_start(out=o, in_=i), reads=reads, writes=writes,
                     dma_sem=semname, extra=extra)

        def mm(out, lhsT, rhs, start, stop, reads, writes, sgc=False):
            return E("pe", lambda e: e.matmul(out, lhsT=lhsT, rhs=rhs, start=start, stop=stop, skip_group_check=sgc),
                     reads=reads, writes=writes)

        def tr(out, in_, reads, writes):
            return E("pe", lambda e: e.transpose(out, in_, cmat[:, 0, :]), reads=reads + ["cmat"], writes=writes)

        def act(out, in_, func, reads, writes, scale=1.0, bias=0.0, accum_out=None):
            def fn(e):
                kw = {}
                if accum_out is not None:
                    kw["accum_out"] = accum_out
                return e.activation(out=out, in_=in_, func=func, bias=bias, scale=scale, **kw)
            return E("act", fn, reads=reads, writes=writes)

        def ts(eng, out, in0, s1, s2, op0, op1, reads, writes):
            def fn(e):
                if op1 is None:
                    return e.tensor_scalar(out=out, in0=in0, scalar1=s1, scalar2=None, op0=op0)
                return e.tensor_scalar(out=out, in0=in0, scalar1=s1, scalar2=s2, op0=op0, op1=op1)
            return E(eng, fn, reads=reads, writes=writes)

        def tt(eng, out, in0, in1, op, reads, writes):
            return E(eng, lambda e: e.tensor_tensor(out=out, in0=in0, in1=in1, op=op), reads=reads, writes=writes)

        def stt(out, in0, scalar, in1, op0, op1, reads, writes):
            return E("dve", lambda e: e.scalar_tensor_tensor(out=out, in0=in0, scalar=scalar, in1=in1, op0=op0, op1=op1),
                     reads=reads, writes=writes)

        def copy(eng, out, in_, reads, writes):
            if eng == "act":
                return act(out, in_, AF.Copy, reads, writes)
            return E(eng, lambda e: e.tensor_copy(out=out, in_=in_), reads=reads, writes=writes)

        def memset(eng, ap, val, writes):
            return E(eng, lambda e: e.memset(ap, val), writes=writes)

        def rstd_from(ms_ap, out_ap, cells_in, cells_out):
            act(out_ap, ms_ap, AF.Ln, cells_in, cells_out)
            return act(out_ap, out_ap, AF.Exp, cells_out, cells_out, scale=-0.5)

        conv_tok = {}

        def wsrc(l, t):
            if t >= 46:
                W, r0, c0 = w_sw_d[l], 0, (t - 46) * 512
            elif t < T_KV:
                W, r0, c0 = w_in_d[l], 0, (t - T_IN) * 512
            elif t < T_PA:
                W, r0, c0 = w_kv_d[l], 0, (t - T_KV) * 512
            elif t < T_PL:
                W, r0, c0 = w_pa_d[l], 0, (t - T_PA) * 512
            elif t < T_PM:
                W, r0, c0 = w_pl_d[l], 0, (t - T_PL) * 512
            elif t < T_OUT:
                W, r0, c0 = w_pm_d[l], 0, (t - T_PM) * 512
            elif t < T_UP:
                W, r0, c0 = w_out_d[l], 0, (t - T_OUT) * 512
            elif t < T_DN:
                W, r0, c0 = w_up_d[l], 0, (t - T_UP) * 512
            else:
                i = t - T_DN
                W, r0, c0 = w_dn_d[l], (i % 4) * 1024, (i // 4) * 512
            return W[r0:r0 + 1024, c0:c0 + 512].rearrange("(c p) n -> p c n", p=P)

        use_order = [2, 48, 3, 49, 4, 5, 0, 46, 1, 47, 22, 12, 23, 13, 6, 8, 7, 9, "rg", 24, 14, 25, 15, 18, 19, 20, 21, 10, 11, 26, 16,
                     27, 17, 28, 29] + list(range(30, 46))
        NGRP = 8
        grp_of = {}
        for i, t in enumerate(use_order):
            grp_of[t] = min(i * NGRP // len(use_order), NGRP - 1)

        def emit_conv(l, items):
            for t in items:
                g = grp_of[t]
                sname = f"cv{g}"
                if t == "rg":
                    o = rgsc_d[l].rearrange("p (a n d) -> p a n d", a=2, n=8)
                    i = rg_d[l].rearrange("a n c d -> c a n d")
                else:
                    o = wsc_d[l, t].rearrange("p (c n) -> p c n", c=8)
                    i = wsrc(l, t)
                op = dma("pool", o, i, [], [], sname)
                conv_tok[(l, t)] = op
                conv_last[(l, g)] = op

        conv_last = {}

        def conv_dep(l, t):
            return conv_last[(l, grp_of[t])]

        def load_w(l, t):
            s = rr["slot"]
            rr["slot"] = (s + 1) % NSLOT
            dma("sp", WS[s][:, :, :], wsc_d[l, t].rearrange("p (c n) -> p c n", c=8), [], [f"ws{s}"], f"wl{s}",
                extra=[conv_dep(l, t)])
            return WS[s], [f"ws{s}"]

        cst_ops = []
        cst_ops.append(dma("sp", pv[:, :, :], pvec_d.rearrange("l p v -> p l v"), [], ["pv"], "cst"))
        cst_ops.append(dma("sp", lamv[:, :, :], lamv_d.rearrange("l p v -> p l v"), [], ["xc"], "cst"))
        cst_ops.append(dma("sp", cmat_f[:, :, :], cmat_d.rearrange("k p n -> p k n"), [], ["cmat_f"], "cst"))
        last_cst = cst_ops[-1]
        pg.last_write["pv"] = last_cst
        pg.last_write["xc"] = last_cst
        pg.last_write["cmat_f"] = last_cst

        first_l = layers[0]
        emit_conv(first_l, use_order)
        pending_conv = []
        for l in layers[1:]:
            pending_conv.append((l, list(use_order)))

        copy("act", cmat[:, :, :], cmat_f[:, :, :], ["cmat_f"], ["cmat"])
        memset("dve", ones_bf[:, :], 1.0, ["ones"])
        memset("dve", small[:, 60:64], -0.5, ["nhalf"])
        memset("pool", VC[:, :, :, 128:130], 1.0, [f"VC{b}" for b in range(NB)])

        for l in layers:
            lam_init = 0.8 - 0.6 * math.exp(-0.3 * l)
            dl = der[:, l, :]
            pl_ = pv[:, l, :]
            cd = [f"der{l}"]
            act(dl[:, 0:8], pl_[:, 104:112], AF.Exp, ["pv"], cd, scale=-1.0)
            act(dl[:, 0:8], dl[:, 0:8], AF.Ln, cd, cd, scale=1.0, bias=1.0)
            ts("dve", dl[:, 8:16], dl[:, 0:8], -8.0, None, ALU.mult, None, cd, cd)
            ts("dve", dl[:, 0:8], dl[:, 0:8], -4.0, None, ALU.mult, None, cd, cd)
            ts("dve", dl[:, 16:40], pl_[:, 24:48], 0.5, None, ALU.mult, None, ["pv"] + cd, cd)
            ts("dve", dl[:, 40:56], pl_[:, 88:104], 0.5, None, ALU.mult, None, ["pv"] + cd, cd)
            lv = lamv[:, l, :]
            tt("dve", t1_t[0][:, 0:64], lv[:, 0:64], lv[:, 64:128], ALU.mult, ["xc"], ["t1_0"])
            E("dve", lambda e, o=dl[:, 58:59]: e.reduce_sum(out=o, in_=t1_t[0][:, 0:64], axis=AX.X), reads=["t1_0"], writes=cd)
            tt("dve", t1_t[0][:, 0:64], lv[:, 128:192], lv[:, 192:256], ALU.mult, ["xc"] + cd, ["t1_0"])
            E("dve", lambda e, o=dl[:, 59:60]: e.reduce_sum(out=o, in_=t1_t[0][:, 0:64], axis=AX.X), reads=["t1_0"], writes=cd)
            act(dl[:, 60:62], dl[:, 58:60], AF.Exp, cd, cd)
            tt("dve", dl[:, 56:57], dl[:, 61:62], dl[:, 60:61], ALU.subtract, cd, cd)
            ts("dve", dl[:, 56:57], dl[:, 56:57], -lam_init, None, ALU.add, None, cd, cd)
            ts("dve", dl[:, 57:58], pl_[:, 112:113], 1.0 - lam_init, None, ALU.mult, None, ["pv"] + cd, cd)

        def prenorm(l, gcol0):
            for b in range(4):
                act(junk[:, :], xt[:, b, :], AF.Square, ["xt"], ["ssq"], accum_out=small[:, b:b + 1])
            ts("dve", small[:, 8:12], small[:, 0:4], 1.0 / D, EPS, ALU.mult, ALU.add, ["ssq"], ["ms"])
            rstd_from(small[:, 8:12], small[:, 12:16], ["ms"], ["rstd"])
            for b in range(4):
                ts("dve", xn[:, b, :], xt[:, b, :], small[:, 12 + b:13 + b], None, ALU.mult, None, ["xt", "rstd"], c_xn(b))
            for c in range(8):
                half = 0
                pc = "ps7"
                for b in range(4):
                    tr(pT[:, half * 512 + b * P: half * 512 + (b + 1) * P], xn[:, b, c * P:(c + 1) * P], c_xn(b), [pc])
                act(hT[:, c, :], pT[:, half * 512:(half + 1) * 512], AF.Copy, [pc, "pv"], c_hT(c),
                    scale=pv[:, l, gcol0 + c:gcol0 + c + 1])

        def inproj_chunk(wt, wc, j, bank, N=TT, rhs=None, rcells=None):
            rhs = hT if rhs is None else rhs
            for c in range(8):
                mm(ps[bank][:, 0:N], wt[:, c, j * P:(j + 1) * P], rhs[:, c, :],
                   c == 0, c == 7, wc + (c_hT(c) if rcells is None else rcells), BANKC[bank])

        def rope(bank, dest, dcells, wts, wsc_, j):
            i = rr["qb"]
            rr["qb"] ^= 1
            bB = bankB()
            inproj_chunk(wts, wsc_, j, bB)
            tt("dve", t1_t[i][:, :], ps[bank][:, :], ropet[:, 0, :], ALU.mult, BANKC[bank] + ["ropet"], [f"t1_{i}"])
            tt("dve", t2_t[i][:, :], ps[bB][:, :], ropet[:, 1, :], ALU.mult, BANKC[bB] + ["ropet"], [f"t2_{i}"])
            tt("pool", dest, t1_t[i][:, :], t2_t[i][:, :], ALU.add, [f"t1_{i}", f"t2_{i}"], dcells)

        def attention(l, t):
            dl = der[:, l, :]
            for h in range(NH):
                nkb = 4 * (t + 1)
                for kb in range(nkb):
                    j = kb - 4 * t
                    q0 = P * j if j > 0 else 0
                    pi = rr["pt"]
                    rr["pt"] = (pi + 1) % 2
                    sb_ = pi * 2
                    kc = [f"KT{h}_{kb // 4}"]
                    mm(ps[sb_][:, q0:TT], KT[0:64, h, kb * P:(kb + 1) * P], QT[0:64, h, q0:TT], True, True,
                       kc + c_QT(h), BANKC[sb_])
                    mm(ps[sb_ + 1][:, q0:TT], KT[64:128, h, kb * P:(kb + 1) * P], QT[64:128, h, q0:TT], True, True,
                       kc + c_QT(h), BANKC[sb_ + 1])
                    for m in range(2):
                        act(PT[pi][m][:, q0:TT], ps[sb_ + m][:, q0:TT], AF.Exp, BANKC[sb_ + m], [f"pt{pi}_{m}"], scale=0.125)
                        if j >= 0:
                            memset("pool", PT[pi][m][64:128, q0:q0 + 64], 0.0, [f"pt{pi}_{m}"])
                    for qb in range(max(j, 0), 4):
                        for m in range(2):
                            oi = qb * 2 + m
                            bank = 4 + oi // 3
                            off = (oi % 3) * 130
                            mm(ps[bank][:, off:off + 130], PT[pi][m][:, qb * P:(qb + 1) * P], VC[:, kb, h, :],
                               (kb == 0 and oi % 3 == 0), (kb == 4 * t + qb), [f"pt{pi}_{m}", f"VC{kb}"], BANKC[bank], sgc=True)
                for qb in range(4):
                    if True:
                        oi1, oi2 = qb * 2, qb * 2 + 1
                        O1 = ps[4 + oi1 // 3][:, (oi1 % 3) * 130:(oi1 % 3) * 130 + 130]
                        O2 = ps[4 + oi2 // 3][:, (oi2 % 3) * 130:(oi2 % 3) * 130 + 130]
                        k = rr["o"]
                        rr["o"] ^= 1
                        st = small[:, 16 + 8 * k:24 + 8 * k]
                        sc = [f"ast{k}"]
                        E("dve", lambda e, o=st[:, 0:1], i=O1[:, 128:129]: e.reciprocal(out=o, in_=i), reads=BANKC[4 + oi1 // 3], writes=sc)
                        E("dve", lambda e, o=st[:, 1:2], i=O2[:, 128:129]: e.reciprocal(out=o, in_=i), reads=BANKC[4 + oi2 // 3], writes=sc)
                        tt("dve", st[:, 2:3], st[:, 1:2], dl[:, 56:57], ALU.mult, sc + [f"der{l}"], sc)
                        ts("dve", tt_t[k][:, :], O2[:, 0:P], st[:, 2:3], None, ALU.mult, None, BANKC[4 + oi2 // 3] + sc, [f"tt{k}"])
                        stt(o_t[k][:, :], O1[:, 0:P], st[:, 0:1], tt_t[k][:, :], ALU.mult, ALU.add, BANKC[4 + oi1 // 3] + [f"tt{k}"] + sc, [f"o{k}"])
                        act(junk[:, 0:P], o_t[k][:, :], AF.Square, [f"o{k}"], sc, accum_out=st[:, 3:4])
                        ts("dve", st[:, 4:5], st[:, 3:4], 1.0 / P, EPS, ALU.mult, ALU.add, sc, sc)
                        rstd_from(st[:, 4:5], st[:, 5:6], sc, sc)
                        ts("dve", ybf[:, qb, :], o_t[k][:, :], st[:, 5:6], None, ALU.mult, None, [f"o{k}"] + sc, [f"ybf{qb}"])
                half = 0
                pc = "ps7"
                for qb in range(4):
                    tr(pT[:, half * 512 + qb * P: half * 512 + (qb + 1) * P], ybf[:, qb, :], [f"ybf{qb}"], [pc])
                ts("dve", yT[:, h, :], pT[:, half * 512:(half + 1) * 512], dl[:, 57:58], None, ALU.mult, None,
                   [pc, f"der{l}"], c_yT(h))

        def proj_gate(l, bidx, wp, first, last):
            dl = der[:, l, :]
            for hh in range(2):
                wtp, wpc = load_w(l, wp + hh)
                wtg, wgc = load_w(l, 12 + 2 * bidx + hh)
                for j in range(4):
                    jc = hh * 4 + j
                    bA = bankA()
                    bB = bankB()
                    for c in range(8):
                        mm(ps[bA][:, :], wtp[:, c, j * P:(j + 1) * P], yT[:, c, :], c == 0, c == 7, wpc + c_yT(c), BANKC[bA])
                    for c in range(8):
                        mm(ps[bB][:, :], wtg[:, c, j * P:(j + 1) * P], hT[:, c, :], c == 0, c == 7, wgc + c_hT(c), BANKC[bB])
                    i = rr["t"]
                    rr["t"] ^= 1
                    col = 16 + bidx * 8 + jc
                    act(t1_t[i][:, :], ps[bB][:, :], AF.Tanh, BANKC[bB] + [f"der{l}"], [f"t1_{i}"], scale=0.5, bias=dl[:, col:col + 1])
                    if first:
                        stt(m_f[:, jc, :], t1_t[i][:, :], 1.0, ps[bA][:, :], ALU.add, ALU.mult, [f"t1_{i}"] + BANKC[bA], c_mf(jc))
                    else:
                        stt(t2_t[i][:, :], t1_t[i][:, :], 1.0, ps[bA][:, :], ALU.add, ALU.mult, [f"t1_{i}"] + BANKC[bA], [f"t2_{i}"])
                        if last:
                            tt("dve", m_bf[:, jc, :], m_f[:, jc, :], t2_t[i][:, :], ALU.add, c_mf(jc) + [f"t2_{i}"], c_mbf(jc))
                        else:
                            tt("dve", m_f[:, jc, :], m_f[:, jc, :], t2_t[i][:, :], ALU.add, c_mf(jc) + [f"t2_{i}"], c_mf(jc))

        def lru(l, t):
            dl = der[:, l, :]
            pl_ = pv[:, l, :]
            dma("sp", rgw[:, :, :, :], rgsc_d[l].rearrange("p (a n d) -> p a n d", a=2, n=8), [], ["rgw"], "rgl",
                extra=[conv_dep(l, "rg")])
            wts = {}
            for hh in range(2):
                wts[("x", hh)] = load_w(l, 6 + hh)
                wts[("y", hh)] = load_w(l, 8 + hh)
                for j in range(4):
                    c = hh * 4 + j
                    wt, wc = wts[("x", hh)]
                    bA = bankA()
                    inproj_chunk(wt, wc, j, bA)
                    if t == 0:
                        memset("pool", zxe[:, 0:3], 0.0, ["zxe"])
                    else:
                        copy("pool", zxe[:, 0:3], cz[:, c, :], [f"cz{c}"], ["zxe"])
                    act(zxe[:, 3:TT + 3], ps[bA][:, :], AF.Copy, BANKC[bA], ["zxe"])
                    copy("pool", cz[:, c, :], zxe[:, TT:TT + 3], ["zxe"], [f"cz{c}"])
                    ts("dve", xc[:, :], zxe[:, 3:TT + 3], pl_[:, 48 + 24 + c:48 + 25 + c], pl_[:, 80 + c:81 + c], ALU.mult, ALU.add,
                       ["zxe", "pv"], ["xc"])
                    for jj in (2, 1, 0):
                        stt(xc[:, :], zxe[:, jj:TT + jj], pl_[:, 48 + 8 * jj + c:48 + 8 * jj + c + 1], xc[:, :], ALU.mult, ALU.add,
                            ["zxe", "xc", "pv"], ["xc"])
                    copy("pool", xcb[:, :], xc[:, :], ["xc"], ["xcb"])
                    bB = bankB()
                    mm(ps[bB][:, :], rgw[:, 0, c, :], xcb[:, :], True, True, ["rgw", "xcb"], BANKC[bB])
                    bC = bankB()
                    mm(ps[bC][:, :], rgw[:, 1, c, :], xcb[:, :], True, True, ["rgw", "xcb"], BANKC[bC])
                    act(thr[:, :], ps[bB][:, :], AF.Tanh, BANKC[bB] + [f"der{l}"], ["t1_0"], scale=0.5, bias=dl[:, 40 + c:41 + c])
                    act(thi[:, :], ps[bC][:, :], AF.Tanh, BANKC[bC] + [f"der{l}"], ["t1_1"], scale=0.5, bias=dl[:, 48 + c:49 + c])
                    act(a_t[:, :], thr[:, :], AF.Exp, ["t1_0", f"der{l}"], ["t2_0"], scale=dl[:, c:c + 1], bias=dl[:, c:c + 1])
                    act(a2_t[:, :], thr[:, :], AF.Exp, ["t1_0", f"der{l}"], ["t2_1"], scale=dl[:, 8 + c:9 + c], bias=dl[:, 8 + c:9 + c])
                    ts("dve", a2_t[:, :], a2_t[:, :], -1.0, 1.0, ALU.mult, ALU.add, ["t2_1"], ["t2_1"])
                    act(a2_t[:, :], a2_t[:, :], AF.Sqrt, ["t2_1"], ["t2_1"], scale=0.25)
                    stt(thi[:, :], thi[:, :], 1.0, xc[:, :], ALU.add, ALU.mult, ["t1_1", "xc"], ["t1_1"])
                    tt("pool", thi[:, :], thi[:, :], a2_t[:, :], ALU.mult, ["t1_1", "t2_1"], ["t1_1"])
                    init = 0.0 if t == 0 else hcar[:, c:c + 1]
                    E("dve", lambda e, i_=init: e.tensor_tensor_scan(out=hs_t[:, :], data0=a_t[:, :], data1=thi[:, :], initial=i_,
                                                                    op0=ALU.mult, op1=ALU.add),
                      reads=["t2_0", "t1_1", f"hcar{c}"], writes=["hs_t"])
                    copy("pool", hcar[:, c:c + 1], hs_t[:, TT - 1:TT], ["hs_t"], [f"hcar{c}"])
                    wt, wc = wts[("y", hh)]
                    bA = bankA()
                    inproj_chunk(wt, wc, j, bA)
                    act(gz_t[:, :], ps[bA][:, :], AF.Gelu_apprx_tanh, BANKC[bA], ["gz_t"])
                    tt("dve", yT[:, c, :], hs_t[:, :], gz_t[:, :], ALU.mult, ["hs_t", "gz_t"], c_yT(c))

        def mem_kv(l, s):
            dma("sp", mt_f[:, :, :], mem_d[s * MEM:(s + 1) * MEM, :].rearrange("(b p) d -> p b d", p=P), [], c_mtf, "meml")
            for b in range(2):
                act(junk[:, :], mt_f[:, b, :], AF.Square, c_mtf, ["ssqm"], accum_out=small[:, 32 + b:33 + b])
            ts("dve", small[:, 34:36], small[:, 32:34], 1.0 / D, EPS, ALU.mult, ALU.add, ["ssqm"], ["msm"])
            rstd_from(small[:, 34:36], small[:, 36:38], ["msm"], ["rstdm"])
            for b in range(2):
                ts("dve", memn[:, b, :], mt_f[:, b, :], small[:, 36 + b:37 + b], None, ALU.mult, None, c_mtf + ["rstdm"], c_memn)
            for c in range(8):
                half = 0
                pc = "ps7"
                for b in range(2):
                    tr(pT[:, half * 512 + b * P: half * 512 + (b + 1) * P], memn[:, b, c * P:(c + 1) * P], c_memn, [pc])
                act(memT[:, c, :], pT[:, half * 512:half * 512 + MEM], AF.Copy, [pc, "pv"], c_memT,
                    scale=pv[:, l, 16 + c:17 + c])
            for hh in range(2):
                wt, wc = load_w(l, T_KV + hh)
                for j in range(4):
                    jc = hh * 4 + j
                    bA = bankA()
                    inproj_chunk(wt, wc, j, bA, N=MEM, rhs=memT, rcells=c_memT)
                    copy(alt(), KmT[:, jc, :], ps[bA][:, 0:MEM], BANKC[bA], ["KmT"])
            for hh in range(2):
                wt, wc = load_w(l, T_KV + 2 + hh)
                for b in range(2):
                    bA = bankA()
                    for c in range(8):
                        mm(ps[bA][:, :], memT[:, c, b * P:(b + 1) * P], wt[:, c, :], c == 0, c == 7, wc + c_memT, BANKC[bA])
                    copy(alt(), Vm[:, b, hh * 512:(hh + 1) * 512], ps[bA][:, :], BANKC[bA], ["Vm"])

        def mem_attn(l):
            for hh in range(2):
                wt, wc = load_w(l, 10 + hh)
                for j in range(4):
                    jc = hh * 4 + j
                    bA = bankA()
                    inproj_chunk(wt, wc, j, bA)
                    copy(alt(), QT[:, jc, :], ps[bA][:, :], BANKC[bA], c_QT(jc))
            for g in range(4):
                pis = []
                for mb in range(2):
                    bk = mb
                    for dc in range(2):
                        mm(ps[bk][:, :], KmT[:, 2 * g + dc, mb * P:(mb + 1) * P], QT[:, 2 * g + dc, :], dc == 0, dc == 1,
                           ["KmT"] + c_QT(2 * g + dc), BANKC[bk])
                    act(PT[mb][0][:, :], ps[bk][:, :], AF.Exp, BANKC[bk], [f"pt{mb}_0"], scale=1.0 / 16.0)
                for mb in range(2):
                    mm(ps[2][:, :], ones_bf[:, :], PT[mb][0][:, :], mb == 0, mb == 1, ["ones", f"pt{mb}_0"], BANKC[2])
                E("dve", lambda e: e.reciprocal(out=t1_t[0][:, :], in_=ps[2][:, :]), reads=BANKC[2], writes=["t1_0"])
                for vc in range(2):
                    bk = 4 + vc
                    for mb in range(2):
                        mm(ps[bk][:, :], Vm[:, mb, g * 256 + vc * P: g * 256 + (vc + 1) * P], PT[mb][0][:, :], mb == 0, mb == 1,
                           ["Vm", f"pt{mb}_0"], BANKC[bk])
                    tt("dve", yT[:, 2 * g + vc, :], ps[bk][:, :], t1_t[0][:, :], ALU.mult, BANKC[bk] + ["t1_0"], c_yT(2 * g + vc))

        def post(l, gidx, eps, kind):
            dma("sp", gp[:, :], gpost_d[l, gidx], [], ["hs_t", "gz_t"], "gpl")
            for half in range(2):
                banks = [0, 1, 2, 3] if half == 0 else [4, 5, 6, 0]
                if kind == "out":
                    wt, wc = load_w(l, T_OUT + half)
                    for b in range(4):
                        for c in range(8):
                            mm(ps[banks[b]][:, :], m_bf[:, c, b * P:(b + 1) * P], wt[:, c, :], c == 0, c == 7,
                               wc + c_mbf(c), BANKC[banks[b]])
                else:
                    for kq in range(4):
                        wt, wc = load_w(l, T_DN + half * 4 + kq)
                        for b in range(4):
                            for c in range(8):
                                f = kq * 8 + c
                                mm(ps[banks[b]][:, :], aT[:, f, b * P:(b + 1) * P], wt[:, c, :], (kq == 0 and c == 0),
                                   (kq == 3 and c == 7), wc + c_aT(f), BANKC[banks[b]])
                for b in range(4):
                    copy("dve", o_raw[:, b, half * 512:(half + 1) * 512], ps[banks[b]][:, :], BANKC[banks[b]], c_oraw_b(b))
                    act(junk[:, 0:512], o_raw[:, b, half * 512:(half + 1) * 512], AF.Square, c_oraw_b(b), ["pssq"],
                        accum_out=small[:, 40 + half * 4 + b:41 + half * 4 + b])
            tt("dve", small[:, 48:52], small[:, 40:44], small[:, 44:48], ALU.add, ["pssq"], ["pms"])
            ts("dve", small[:, 48:52], small[:, 48:52], 1.0 / D, eps, ALU.mult, ALU.add, ["pms"], ["pms"])
            rstd_from(small[:, 48:52], small[:, 52:56], ["pms"], ["prstd"])
            for b in range(4):
                stt(o_raw[:, b, :], o_raw[:, b, :], small[:, 52 + b:53 + b], gp[:, :], ALU.mult, ALU.mult,
                    c_oraw_b(b) + ["prstd", "hs_t", "gz_t"], c_oraw_b(b))
                tt("dve", xt[:, b, :], xt[:, b, :], o_raw[:, b, :], ALU.add, ["xt"] + c_oraw_b(b), ["xt"])

        def mlp_up(l):
            for ti in range(8):
                wt, wc = load_w(l, T_UP + ti)
                for j in range(4):
                    f = ti * 4 + j
                    bA = bankA()
                    inproj_chunk(wt, wc, j, bA)
                    i = rr["t"]
                    rr["t"] ^= 1
                    act(t1_t[i][:, :], ps[bA][:, :], AF.Relu, BANKC[bA], [f"t1_{i}"])
                    tt("pool" if (f % 2) else "dve", aT[:, f, :], t1_t[i][:, :], t1_t[i][:, :], ALU.mult, [f"t1_{i}"], c_aT(f))

        for s in range(nseq):
            for li, l in enumerate(layers):
                src = x_d if li == 0 else xmid_d
                dst = out_d if li == len(layers) - 1 else xmid_d
                for t in range(NT):
                    if pending_conv and s == 0 and li == 0:
                        pl, items = pending_conv[0]
                        n = (len(items) + (NT - t) - 1) // (NT - t)
                        emit_conv(pl, items[:n])
                        del items[:n]
                        if not items:
                            pending_conv.pop(0)
                    r0 = s * S + t * TT
                    dcell = f"xm{s}_{t}"
                    dma("sp", xt[:, :, :], src[r0:r0 + TT, :].rearrange("(b p) d -> p b d", p=P),
                        [dcell] if li > 0 else [], ["xt"], "xld")
                    dma("sp", ropet[:, :, :], rope_d[:, :, t * TT:(t + 1) * TT].rearrange("k p s -> p k s"), [], ["ropet"], "rpl")
                    import os
                    DS = int(os.environ.get("DEBUG_STOP", "99"))
                    if DS >= 1:
                        prenorm(l, 0)
                    for hh in range(2):
                        if DS < 2:
                            break
                        wt, wc = load_w(l, 2 + hh)
                        wts, wsc_ = load_w(l, 48 + hh)
                        for j in range(4):
                            h = hh * 4 + j
                            bA = bankA()
                            inproj_chunk(wt, wc, j, bA)
                            rope(bA, KT[:, h, t * TT:(t + 1) * TT], [f"KT{h}_{t}"], wts, wsc_, j)
                    for hh in range(2):
                        if DS < 3:
                            break
                        wt, wc = load_w(l, 4 + hh)
                        for b in range(4):
                            bA = bankA()
                            for c in range(8):
                                mm(ps[bA][:, :], hT[:, c, b * P:(b + 1) * P], wt[:, c, :], c == 0, c == 7, wc + c_hT(c), BANKC[bA])
                            copy(alt(), VC[:, t * 4 + b, hh * 4:(hh + 1) * 4, 0:P],
                                 ps[bA][:, :].rearrange("p (h v) -> p h v", h=4), BANKC[bA], [f"VC{t * 4 + b}"])
                    for hh in range(2):
                        if DS < 4:
                            break
                        wt, wc = load_w(l, 0 + hh)
                        wts, wsc_ = load_w(l, 46 + hh)
                        for j in range(4):
                            h = hh * 4 + j
                            bA = bankA()
                            inproj_chunk(wt, wc, j, bA)
                            rope(bA, QT[:, h, :], c_QT(h), wts, wsc_, j)
                    if DS >= 5:
                        attention(l, t)
                    if DS >= 6:
                        proj_gate(l, 0, T_PA, True, False)
                    if DS >= 7:
                        lru(l, t)
                    if DS >= 8:
                        proj_gate(l, 1, T_PL, False, False)
                    if DS >= 9:
                        if t == 0:
                            mem_kv(l, s)
                        mem_attn(l)
                    if DS >= 10:
                        proj_gate(l, 2, T_PM, False, True)
                    if DS >= 11:
                        post(l, 0, 4.0 * EPS, "out")
                    if DS >= 12:
                        prenorm(l, 8)
                        mlp_up(l)
                    if DS >= 13:
                        post(l, 1, EPS, "dn")
                    dma("sp", dst[r0:r0 + TT, :].rearrange("(b p) d -> p b d", p=P), xt[:, :, :], ["xt"],
                        [dcell] if li < len(layers) - 1 else ["outc"], "xst", extra=[pg.last_write["xst_prev"]] if "xst_prev" in pg.last_write else [])
                    pg.last_write["xst_prev"] = pg.ops[-1]

        final_store = pg.ops[-1]
        dma_cnt = pg.finalize()

        engmap = {"pe": "tensor", "act": "scalar", "dve": "vector", "pool": "gpsimd", "sp": "sync"}
        with nc.Block() as block:
            def make(engname):
                myops = [op for op in pg.ops if op.eng == engname]

                def body(e):
                    for op in myops:
                        for d in op.waits:
                            if d.dma_sem is not None:
                                e.wait_ge(sems[d.dma_sem], d.val)
                            else:
                                e.wait_ge(sems["e_" + d.eng], d.val)
                        ins = op.fn(e)
                        if op.dma_sem is not None:
                            ins.then_inc(sems[op.dma_sem], 16)
                        elif op.signal:
                            ins.then_inc(sems["e_" + op.eng], 1)
                    if engname == "sp":
                        e.wait_ge(sems["xst"], 16 * dma_cnt["xst"])
                return body

            for en in Prog.ENGS:
                getattr(block, engmap[en])(make(en))
    return nc


def _pack_small(inputs, L, S):
    f = np.float32

    def fm(v):
        return np.asarray(v, f).reshape(8, P).T

    pvec = np.zeros((L, P, NV), f)
    lamv = np.zeros((L, P, 256), f)
    gpost = np.zeros((L, 2, P, D), f)
    for l in range(L):
        pvec[l, :, 0:8] = fm(inputs["norm_mix_pre"][l])
        pvec[l, :, 8:16] = fm(inputs["norm_mlp_pre"][l])
        pvec[l, :, 16:24] = fm(inputs["norm_mem"][l])
        for b in range(3):
            pvec[l, :, 24 + 8 * b:32 + 8 * b] = fm(inputs["b_gate"][l][b])
        for j in range(4):
            pvec[l, :, 48 + 8 * j:56 + 8 * j] = fm(inputs["conv_w"][l][j])
        pvec[l, :, 80:88] = fm(inputs["conv_b"][l])
        pvec[l, :, 88:96] = fm(inputs["rg_ba"][l])
        pvec[l, :, 96:104] = fm(inputs["rg_bi"][l])
        pvec[l, :, 104:112] = fm(inputs["rg_lambda"][l])
        pvec[l, :, 112] = np.asarray(inputs["diff_subln"][l], f)
        for i, k in enumerate(["lambda_q1", "lambda_k1", "lambda_q2", "lambda_k2"]):
            lamv[l, :, 64 * i:64 * (i + 1)] = np.asarray(inputs[k][l], f)[None, :]
        gpost[l, 0] = np.asarray(inputs["norm_mix_post"][l], f)[None, :]
        gpost[l, 1] = np.asarray(inputs["norm_mlp_post"][l], f)[None, :]
    half = 32
    pos = np.arange(S, dtype=f)
    inv = (1.0 / (f(10000.0) ** (np.arange(half, dtype=f) / f(half)))).astype(f)
    ang = (pos[:, None] * inv[None, :]).astype(f)
    cos = np.cos(ang).astype(f).T
    sin = np.sin(ang).astype(f).T
    rope = np.zeros((2, P, S), f)
    for p in range(P):
        rope[0, p] = cos[p % 32]
        rope[1, p] = -sin[p % 32] if (p % 64) < 32 else sin[p % 32]
    cmat = np.zeros((2, P, P), f)
    cmat[0] = np.eye(P, dtype=f)
    for m in range(P):
        k = m + 32 if (m % 64) < 32 else m - 32
        cmat[1, k, m] = 1.0
    rg = np.stack([np.asarray(inputs["rg_wa"], f), np.asarray(inputs["rg_wi"], f)], axis=1)
    return pvec, lamv, gpost, rope, cmat, rg


def run(inputs, ncores, nseq, S, layer_groups, L):
    pvec, lamv, gpost, rope, cmat, rg = _pack_small(inputs, L, S)
    f = np.float32
    x = np.ascontiguousarray(np.asarray(inputs["x"], f))
    mem = np.ascontiguousarray(np.asarray(inputs["mem"], f))
    cols = np.arange(2048)
    partner = np.where((cols % 64) < 32, cols + 32, cols - 32)
    w_sw = np.ascontiguousarray(np.asarray(inputs["w_in"], f)[:, :, partner])
    shared = {
        "w_sw": w_sw,
        "w_in": np.ascontiguousarray(inputs["w_in"], f), "w_kv": np.ascontiguousarray(inputs["w_kv_mem"], f),
        "w_pa": np.ascontiguousarray(inputs["w_proj_attn"], f), "w_pl": np.ascontiguousarray(inputs["w_proj_lru"], f),
        "w_pm": np.ascontiguousarray(inputs["w_proj_mem"], f), "w_out": np.ascontiguousarray(inputs["w_out"], f),
        "w_up": np.ascontiguousarray(inputs["w_mlp_up"], f), "w_dn": np.ascontiguousarray(inputs["w_mlp_down"], f),
        "rg_w": np.ascontiguousarray(rg), "pvec": pvec, "gpost": gpost, "lamv": lamv, "rope": rope, "cmat": cmat,
    }
    cur = x
    for layers in layer_groups:
        nc = build(nseq, S, layers, L)
        in_maps = []
        for c in range(ncores):
            m = dict(shared)
            m["x"] = np.ascontiguousarray(cur[c * nseq:(c + 1) * nseq].reshape(nseq * S, D))
            m["mem"] = np.ascontiguousarray(mem[c * nseq:(c + 1) * nseq].reshape(nseq * MEM, D))
            in_maps.append(m)
        res = run_bass_kernel_spmd(nc, in_maps, core_ids=list(range(ncores)))
        cur = np.concatenate([np.asarray(r["out"]).reshape(nseq, S, D) for r in res.results], axis=0)
    return cur.astype(np.float32)


def kernel(**inputs):
    return run(inputs, 8, 4, 2048, [[0, 1]], 2)
```
